# Optimizing a Trainium2 kernel written in Bass

```python
import math
import jax, jax.numpy as jnp
from jax import lax
import numpy as np

D_MODEL = 2048
BATCH = 4
SEQ = 8192
DEPTH = 2
DEC_BATCH = 16
DEC_SEQ = 64
PAST_LEN = 2048

CHUNK = 64
EPS = 1e-6
SSD_HEADS = 32
SSD_HEAD_DIM = 64
D_INNER = SSD_HEADS * SSD_HEAD_DIM
SSD_GROUPS = 4
D_STATE = 128
CONV_W = 4
XBC_DIM = D_INNER + 2 * SSD_GROUPS * D_STATE
GM_CHUNK = 128
GM_GROUPS = 8
GM_GROUP_DIM = 128
GM_WIDTH = GM_GROUPS * GM_GROUP_DIM
D_FF = 5632
PLE_DIM = 256
IN_COLS = D_INNER + XBC_DIM + SSD_HEADS + 2 * GM_WIDTH + 2 * D_MODEL

kernel_name = 'hybrid_ssd_gmlp_streaming_step'


def _rmsnorm(x, g):
    xf = x.astype(jnp.float32)
    y = xf * lax.rsqrt(jnp.mean(xf * xf, axis=-1, keepdims=True) + EPS)
    return (y * g.astype(jnp.float32)).astype(x.dtype)


def _swiglu(h, w_gate, w_up, w_down):
    return (jax.nn.silu(h @ w_gate) * (h @ w_up)) @ w_down


def _causal_dwconv(x, buf, w, b):
    L = x.shape[1]
    xp = jnp.concatenate([buf.astype(x.dtype), x], axis=1)
    y = b
    for k in range(CONV_W):
        y = y + xp[:, k:k + L] * w[k]
    return y, xp[:, xp.shape[1] - (CONV_W - 1):]


def _ssd_scan(x, dt, a, bm, cm, h0):
    bsz, L, H, P = x.shape
    G, N = bm.shape[2], bm.shape[3]
    hg = H // G
    Q = min(CHUNK, L)
    nc = L // Q

    def blocks(t):
        return jnp.moveaxis(t.reshape((bsz, nc, Q) + t.shape[2:]), 1, 0)

    mask = jnp.tril(jnp.ones((Q, Q), dtype=bool))

    def step(h, inp):
        xc, dtc, bc, cc = inp
        cum = jnp.cumsum(dtc * a, axis=1)
        seg = cum[:, :, None, :] - cum[:, None, :, :]
        decay = jnp.exp(jnp.where(mask[None, :, :, None], seg, -jnp.inf))
        cb = jnp.repeat(jnp.einsum('btgn,bsgn->btsg', cc, bc), hg, axis=-1)
        xdt = xc * dtc[..., None]
        y_diag = jnp.einsum('btsh,bshp->bthp', cb * decay, xdt)
        h_grp = h.reshape(bsz, G, hg, P, N)
        y_off = jnp.einsum('btgn,bgkpn->btgkp', cc, h_grp).reshape(bsz, Q, H, P) * jnp.exp(cum)[..., None]
        dec_end = jnp.exp(cum[:, -1:, :] - cum)
        bh = jnp.repeat(bc, hg, axis=2)
        h_new = h * jnp.exp(cum[:, -1])[:, :, None, None] + jnp.einsum('bsh,bshp,bshn->bhpn', dec_end, xdt, bh)
        return h_new, y_diag + y_off

    h_last, ys = lax.scan(step, h0, (blocks(x), blocks(dt), blocks(bm), blocks(cm)))
    y = jnp.moveaxis(ys, 0, 1).reshape(bsz, L, H, P)
    return y, h_last


def _ssd_branch(z, xbc_raw, dt_raw, conv_buf, h0, conv_w, conv_b, dt_bias, a_log, d_skip, norm_g):
    xbc, new_buf = _causal_dwconv(xbc_raw, conv_buf, conv_w, conv_b)
    xbc = jax.nn.silu(xbc).astype(jnp.float32)
    bsz, L, _ = xbc.shape
    gn = SSD_GROUPS * D_STATE
    xs = xbc[..., :D_INNER].reshape(bsz, L, SSD_HEADS, SSD_HEAD_DIM)
    bm = xbc[..., D_INNER:D_INNER + gn].reshape(bsz, L, SSD_GROUPS, D_STATE)
    cm = xbc[..., D_INNER + gn:].reshape(bsz, L, SSD_GROUPS, D_STATE)
    dt = jax.nn.softplus(dt_raw.astype(jnp.float32) + dt_bias.astype(jnp.float32))
    a = -jnp.exp(a_log.astype(jnp.float32))
    y, h_last = _ssd_scan(xs, dt, a, bm, cm, h0.astype(jnp.float32))
    y = y + xs * d_skip.astype(jnp.float32)[:, None]
    yg = (y.reshape(bsz, L, D_INNER) * jax.nn.silu(z.astype(jnp.float32))).reshape(bsz, L, SSD_GROUPS, D_INNER // SSD_GROUPS)
    yg = yg * lax.rsqrt(jnp.mean(yg * yg, axis=-1, keepdims=True) + EPS)
    y = yg.reshape(bsz, L, D_INNER) * norm_g.astype(jnp.float32)
    return y.astype(z.dtype), new_buf, h_last


def _gmlp_branch(uv, ln_g, ln_b, ws, bs):
    uv = jax.nn.gelu(uv)
    u, v = uv[..., :GM_WIDTH], uv[..., GM_WIDTH:]
    vf = v.astype(jnp.float32)
    mu = jnp.mean(vf, axis=-1, keepdims=True)
    var = jnp.mean(jnp.square(vf - mu), axis=-1, keepdims=True)
    vn = ((vf - mu) * lax.rsqrt(var + EPS) * ln_g.astype(jnp.float32) + ln_b.astype(jnp.float32)).astype(v.dtype)
    bsz, L, _ = u.shape
    T = min(GM_CHUNK, L)
    nc = L // T
    mask = jnp.tril(jnp.ones((T, T), dtype=bool))
    w = jnp.where(mask[None], ws[:, :T, :T], 0.0)
    vc = vn.reshape(bsz, nc, T, GM_GROUPS, GM_GROUP_DIM)
    bias = jnp.transpose(bs[:, :T])
    mixed = jnp.einsum('gts,bnsgc->bntgc', w, vc) + bias[None, None, :, :, None]
    return u * mixed.reshape(bsz, L, GM_WIDTH), vn


def _trunk(x, pe, conv_state, ssm_state, w, keep_v):
    new_conv, new_ssm, new_v = [], [], []
    o1 = D_INNER
    o2 = o1 + XBC_DIM
    o3 = o2 + SSD_HEADS
    o4 = o3 + 2 * GM_WIDTH
    for i in range(DEPTH):
        h = _rmsnorm(x, w['ffn1_norm'][i])
        x = x + 0.5 * _swiglu(h, w['ffn1_w_gate'][i], w['ffn1_w_up'][i], w['ffn1_w_down'][i])
        h = _rmsnorm(x, w['mix_norm'][i])
        proj = h @ w['w_in'][i]
        z, xbc, dt_raw = proj[..., :o1], proj[..., o1:o2], proj[..., o2:o3]
        uv, gates = proj[..., o3:o4], proj[..., o4:]
        ya, buf, hs = _ssd_branch(z, xbc, dt_raw, conv_state[i], ssm_state[i], w['conv_w'][i], w['conv_b'][i],
                                  w['dt_bias'][i], w['a_log'][i], w['d_skip'][i], w['ssd_norm'][i])
        yb, vn = _gmlp_branch(uv, w['gm_ln_g'][i], w['gm_ln_b'][i], w['gm_ws'][i], w['gm_bs'][i])
        ga = jax.nn.sigmoid(gates[..., :D_MODEL])
        gb = jax.nn.sigmoid(gates[..., D_MODEL:])
        merged = ga * (ya @ w['w_branch_ssd'][i]) + gb * (yb @ w['w_branch_gmlp'][i])
        x = x + merged @ w['w_out'][i]
        h = _rmsnorm(x, w['ffn2_norm'][i])
        x = x + 0.5 * _swiglu(h, w['ffn2_w_gate'][i], w['ffn2_w_up'][i], w['ffn2_w_down'][i])
        h = _rmsnorm(x, w['ple_norm'][i])
        x = x + (pe[i] @ w['ple_w_proj'][i]) * jax.nn.sigmoid(h @ w['ple_w_gate'][i])
        new_conv.append(buf)
        new_ssm.append(hs)
        if keep_v:
            new_v.append(vn)
    y = _rmsnorm(x, w['final_norm'])
    v_out = jnp.stack(new_v) if keep_v else None
    return y, jnp.stack(new_conv), jnp.stack(new_ssm), v_out


def setup_inputs(seed: int = 0) -> dict:
    key = jax.random.key(seed)
    ks = list(jax.random.split(key, 48))
    cnt = [0]

    def nk():
        k = ks[cnt[0]]
        cnt[0] += 1
        return k

    def nrm(shape, scale):
        return scale * jax.random.normal(nk(), shape, jnp.float32)

    def gain(shape):
        return 1.0 + 0.05 * jax.random.normal(nk(), shape, jnp.float32)

    u = jax.random.uniform(nk(), (DEPTH, SSD_HEADS), jnp.float32)
    dt0 = jnp.exp(u * (math.log(0.1) - math.log(1e-3)) + math.log(1e-3))
    dt_bias = dt0 + jnp.log(-jnp.expm1(-dt0))
    a_log = jnp.log(jax.random.uniform(nk(), (DEPTH, SSD_HEADS), jnp.float32, minval=1.0, maxval=16.0))
    return {
        'x_prompt': nrm((BATCH, SEQ, D_MODEL), 1.0),
        'x_sample': nrm((DEC_BATCH, DEC_SEQ, D_MODEL), 1.0),
        'p_prompt': nrm((DEPTH, BATCH, SEQ, PLE_DIM), 1.0),
        'p_sample': nrm((DEPTH, DEC_BATCH, DEC_SEQ, PLE_DIM), 1.0),
        'state_conv': nrm((DEPTH, DEC_BATCH, CONV_W - 1, XBC_DIM), 1.0),
        'state_ssm': nrm((DEPTH, DEC_BATCH, SSD_HEADS, SSD_HEAD_DIM, D_STATE), 0.1),
        'ffn1_norm': gain((DEPTH, D_MODEL)),
        'ffn1_w_gate': nrm((DEPTH, D_MODEL, D_FF), D_MODEL ** -0.5),
        'ffn1_w_up': nrm((DEPTH, D_MODEL, D_FF), D_MODEL ** -0.5),
        'ffn1_w_down': nrm((DEPTH, D_FF, D_MODEL), D_FF ** -0.5),
        'mix_norm': gain((DEPTH, D_MODEL)),
        'w_in': nrm((DEPTH, D_MODEL, IN_COLS), D_MODEL ** -0.5),
        'conv_w': nrm((DEPTH, CONV_W, XBC_DIM), CONV_W ** -0.5),
        'conv_b': nrm((DEPTH, XBC_DIM), 0.01),
        'dt_bias': dt_bias,
        'a_log': a_log,
        'd_skip': gain((DEPTH, SSD_HEADS)),
        'ssd_norm': gain((DEPTH, D_INNER)),
        'gm_ln_g': gain((DEPTH, GM_WIDTH)),
        'gm_ln_b': nrm((DEPTH, GM_WIDTH), 0.01),
        'gm_ws': nrm((DEPTH, GM_GROUPS, GM_CHUNK, GM_CHUNK), GM_CHUNK ** -0.5),
        'gm_bs': gain((DEPTH, GM_GROUPS, GM_CHUNK)),
        'w_branch_ssd': nrm((DEPTH, D_INNER, D_MODEL), D_INNER ** -0.5),
        'w_branch_gmlp': nrm((DEPTH, GM_WIDTH, D_MODEL), GM_WIDTH ** -0.5),
        'w_out': nrm((DEPTH, D_MODEL, D_MODEL), D_MODEL ** -0.5),
        'ffn2_norm': gain((DEPTH, D_MODEL)),
        'ffn2_w_gate': nrm((DEPTH, D_MODEL, D_FF), D_MODEL ** -0.5),
        'ffn2_w_up': nrm((DEPTH, D_MODEL, D_FF), D_MODEL ** -0.5),
        'ffn2_w_down': nrm((DEPTH, D_FF, D_MODEL), D_FF ** -0.5),
        'ple_norm': gain((DEPTH, D_MODEL)),
        'ple_w_gate': nrm((DEPTH, D_MODEL, D_MODEL), D_MODEL ** -0.5),
        'ple_w_proj': nrm((DEPTH, PLE_DIM, D_MODEL), PLE_DIM ** -0.5),
        'final_norm': gain((D_MODEL,)),
    }


def reference(x_prompt, x_sample, p_prompt, p_sample, state_conv, state_ssm,
              ffn1_norm, ffn1_w_gate, ffn1_w_up, ffn1_w_down,
              mix_norm, w_in, conv_w, conv_b, dt_bias, a_log, d_skip, ssd_norm,
              gm_ln_g, gm_ln_b, gm_ws, gm_bs,
              w_branch_ssd, w_branch_gmlp, w_out,
              ffn2_norm, ffn2_w_gate, ffn2_w_up, ffn2_w_down,
              ple_norm, ple_w_gate, ple_w_proj, final_norm):
    w = {
        'ffn1_norm': ffn1_norm, 'ffn1_w_gate': ffn1_w_gate, 'ffn1_w_up': ffn1_w_up, 'ffn1_w_down': ffn1_w_down,
        'mix_norm': mix_norm, 'w_in': w_in, 'conv_w': conv_w, 'conv_b': conv_b,
        'dt_bias': dt_bias, 'a_log': a_log, 'd_skip': d_skip, 'ssd_norm': ssd_norm,
        'gm_ln_g': gm_ln_g, 'gm_ln_b': gm_ln_b, 'gm_ws': gm_ws, 'gm_bs': gm_bs,
        'w_branch_ssd': w_branch_ssd, 'w_branch_gmlp': w_branch_gmlp, 'w_out': w_out,
        'ffn2_norm': ffn2_norm, 'ffn2_w_gate': ffn2_w_gate, 'ffn2_w_up': ffn2_w_up, 'ffn2_w_down': ffn2_w_down,
        'ple_norm': ple_norm, 'ple_w_gate': ple_w_gate, 'ple_w_proj': ple_w_proj, 'final_norm': final_norm,
    }
    bp = x_prompt.shape[0]
    conv0 = jnp.zeros((DEPTH, bp, CONV_W - 1, XBC_DIM), x_prompt.dtype)
    ssm0 = jnp.zeros((DEPTH, bp, SSD_HEADS, SSD_HEAD_DIM, D_STATE), jnp.float32)
    y_prompt, conv_prompt, ssm_prompt, _ = _trunk(x_prompt, p_prompt, conv0, ssm0, w, False)
    y_sample, conv_sample, ssm_sample, gmlp_v_sample = _trunk(x_sample, p_sample, state_conv, state_ssm, w, True)
    return (y_prompt, y_sample, ssm_prompt, conv_prompt, ssm_sample, conv_sample, gmlp_v_sample)
```

```python
import bisect
import numpy as np
import concourse.bass as bass
import concourse.mybir as mybir
from concourse.bass_utils import run_bass_kernel_spmd

F32 = mybir.dt.float32
BF16 = mybir.dt.bfloat16
AF = mybir.ActivationFunctionType
ALU = mybir.AluOpType

D = 2048
KD = 16
DFF = 5632
KF = 44
H = 32
HP = 64
G = 4
NS = 128
XBC = 3072
GMW = 1024
PLE = 256
DEPTH = 2
SEQ = 8192
EPS = 1e-6
SLOT_ELEMS = 6144
NSLOT = 3


class Sched:
    def __init__(self, nc):
        self.nc = nc
        self.eng = {"pe": nc.tensor, "act": nc.scalar, "dve": nc.vector,
                    "pool": nc.gpsimd, "sp": nc.sync}
        self.sem = {}
        self.cnt = {}
        self.insts = {}
        self.sig_idx = {}
        self.sig_val = {}
        self.waited = {}
        for k in self.eng:
            self.sem[k] = nc.semaphore("s_" + k).__enter__()
            self.cnt[k] = 0
            self.insts[k] = []
            self.sig_idx[k] = []
            self.sig_val[k] = []
            self.waited[k] = {}
        self.dsem = {}
        self.dcnt = {}
        self.last_w = {}
        self.readers = {}
        self.nwaits = 0
        self.last_bar = []
        self.last_bar_d = {}
        self.arena_rd = set()

    def _signal(self, en, idx):
        si = self.sig_idx[en]
        j = bisect.bisect_left(si, idx)
        if j < len(si):
            return self.sig_val[en][j]
        inst = self.insts[en][idx]
        inst.then_inc(self.sem[en], 1)
        self.cnt[en] += 1
        si.append(idx)
        self.sig_val[en].append(self.cnt[en])
        return self.cnt[en]

    def _need(self, en, tok, raw=True, force=False):
        if tok is None:
            return
        if tok[0] == "e":
            _, src, idx = tok
            if src == en and not force:
                if not raw or en == "pe" or idx != len(self.insts[en]) - 1:
                    return
            val = self._signal(src, idx)
            key = ("e", src)
            sem = self.sem[src]
        else:
            _, dk, val = tok
            key = ("d", dk)
            sem = self.dsem[dk]
        if self.waited[en].get(key, 0) >= val:
            return
        self.eng[en].wait_ge(sem, val)
        self.waited[en][key] = val
        self.nwaits += 1

    def _deps(self, en, reads, writes, force=False):
        for r in reads:
            self._need(en, self.last_w.get(r), raw=True, force=force)
        for w in writes:
            self._need(en, self.last_w.get(w), raw=False, force=force)
            rd = self.readers.get(w)
            if rd:
                for t in rd.values():
                    self._need(en, t, raw=False, force=force)

    def op(self, en, fn, reads=(), writes=()):
        self._deps(en, reads, writes)
        inst = fn(self.eng[en])
        idx = len(self.insts[en])
        self.insts[en].append(inst)
        tok = ("e", en, idx)
        for r in reads:
            self.readers.setdefault(r, {})[en] = tok
        for w in writes:
            self.last_w[w] = tok
            self.readers[w] = {}
        return tok

    def set_writer(self, res, tok):
        self.last_w[res] = tok
        self.readers[res] = {}

    def dma(self, q, key, out, in_, reads=(), writes=(), extra=(), after_bar=False, arena_read=False, **kw):
        if key not in self.dsem:
            self.dsem[key] = self.nc.semaphore("d_" + key).__enter__()
            self.dcnt[key] = 0
        self._deps(q, reads, writes, force=True)
        for t in extra:
            self._need(q, t, force=True)
        if after_bar:
            for t in self.last_bar:
                self._need(q, t, force=True)
            for k, c in self.last_bar_d.items():
                self._need(q, ("d", k, c))
        if arena_read:
            self.arena_rd.add(key)
        self.eng[q].dma_start(out=out, in_=in_, **kw).then_inc(self.dsem[key], 16)
        self.dcnt[key] += 16
        tok = ("d", key, self.dcnt[key])
        for r in reads:
            self.readers.setdefault(r, {})["d:" + key] = tok
        for w in writes:
            self.last_w[w] = tok
            self.readers[w] = {}
        return tok

    def barrier(self, engines=("pe", "act", "dve", "pool")):
        toks = []
        for e in engines:
            if self.insts[e]:
                toks.append(("e", e, len(self.insts[e]) - 1))
        for e in engines:
            for t in toks:
                if t[1] != e:
                    self._need(e, t, force=True)
            for k in self.arena_rd:
                self._need(e, ("d", k, self.dcnt[k]))
        self.last_bar = toks
        self.last_bar_d = {k: self.dcnt[k] for k in self.arena_rd}
        return toks

    def finish(self):
        for e in ("pe", "act", "dve", "pool"):
            if self.insts[e]:
                self._need("sp", ("e", e, len(self.insts[e]) - 1), force=True)
        for k, c in self.dcnt.items():
            self._need("sp", ("d", k, c))


IN_Z0 = 0
IN_X0 = 2048
IN_B0 = 4096
IN_C0 = 4608
IN_DT0 = 5120
IN_U0 = 5152
IN_V0 = 6176
IN_GA0 = 7200
IN_GB0 = 9248
IN_COLS = 11296

WDEFS = {
    "f1g": (D, "ffn1_w_gate", 0, DFF), "f1u": (D, "ffn1_w_up", 0, DFF), "f1d": (DFF, "ffn1_w_down", 0, D),
    "f2g": (D, "ffn2_w_gate", 0, DFF), "f2u": (D, "ffn2_w_up", 0, DFF), "f2d": (DFF, "ffn2_w_down", 0, D),
    "iz": (D, "w_in", IN_Z0, 2048), "ix": (D, "w_in", IN_X0, 2048), "ib": (D, "w_in", IN_B0, 512),
    "ic": (D, "w_in", IN_C0, 512), "idt": (D, "w_in", IN_DT0, 32), "iu": (D, "w_in", IN_U0, 1024),
    "iv": (D, "w_in", IN_V0, 1024), "iga": (D, "w_in", IN_GA0, 2048), "igb": (D, "w_in", IN_GB0, 2048),
    "wa": (D, "w_branch_ssd", 0, D), "wb": (GMW, "w_branch_gmlp", 0, D), "wo": (D, "w_out", 0, D),
    "pg": (D, "ple_w_gate", 0, D), "pp": (PLE, "ple_w_proj", 0, D),
}
WSRC = ["ffn1_w_gate", "ffn1_w_up", "ffn1_w_down", "w_in", "w_branch_ssd", "w_branch_gmlp", "w_out",
        "ffn2_w_gate", "ffn2_w_up", "ffn2_w_down", "ple_w_gate", "ple_w_proj"]
WSRC_SHAPE = {"ffn1_w_gate": (D, DFF), "ffn1_w_up": (D, DFF), "ffn1_w_down": (DFF, D), "w_in": (D, IN_COLS),
              "w_branch_ssd": (D, D), "w_branch_gmlp": (GMW, D), "w_out": (D, D),
              "ffn2_w_gate": (D, DFF), "ffn2_w_up": (D, DFF), "ffn2_w_down": (DFF, D),
              "ple_w_gate": (D, D), "ple_w_proj": (PLE, D)}


WIDTH = {"iz": 256, "ix": 256, "wb": 256, "igb": 256}


def slabs_of(name):
    K, _, _, M = WDEFS[name]
    kc = K // 128
    w = WIDTH.get(name, min((SLOT_ELEMS // kc) // 128 * 128, 512))
    out = []
    c = 0
    while c < M:
        ww = min(w, M - c)
        out.append((c, ww))
        c += ww
    return out


V_F1N, V_MXN, V_F2N, V_PLN, V_SSN, V_CB, V_CW = 0, 16, 32, 48, 64, 80, 104
NV = 104 + 96


def build(n_ptiles=16, with_sample=True, stop_after=None, TP=512, layers=DEPTH, cast_names=None, dbg=False):
    nc = bass.Bass("TRN2", target_bir_lowering=False)
    S = Sched(nc)
    dram_in = {}

    def din(name, shape, dt=F32):
        dram_in[name] = nc.dram_tensor(name, list(shape), dt, kind="ExternalInput").ap()
        return dram_in[name]

    def dout(name, shape):
        return nc.dram_tensor(name, list(shape), F32, kind="ExternalOutput").ap()

    xp = din("xp", (SEQ, D))
    ppr = din("ppr", (DEPTH, SEQ, PLE))
    xs = din("xs", (256, D))
    psm = din("psm", (DEPTH, 256, PLE))
    sc = din("sc", (DEPTH, 4, 3, XBC))
    ss = din("ss", (DEPTH, 4, H * HP, NS))
    wsrc = {n: din(n, (DEPTH,) + WSRC_SHAPE[n]) for n in WSRC}
    vecs = din("vecs_fm", (DEPTH, 128, NV))
    fnorm = din("fnorm_fm", (128, KD))
    hvec = din("hvec", (DEPTH, 1, 96))
    lng = din("gm_ln_g", (DEPTH, 1, GMW))
    lnb = din("gm_ln_b", (DEPTH, 1, GMW))
    gws = din("gm_wsT", (DEPTH, 128, 8, 128))
    gbs = din("gm_bs", (DEPTH, 1, 8 * 128))

    yp = dout("yp", (SEQ, D))
    ys = dout("ys", (256, D))
    ssm_p = dout("ssm_p", (DEPTH, H * HP, NS))
    conv_p = dout("conv_p", (DEPTH, 3, XBC))
    ssm_s = dout("ssm_s", (DEPTH, 4, H * HP, NS))
    conv_s = dout("conv_s", (DEPTH, 4, 3, XBC))
    gv = dout("gv", (DEPTH, 256, GMW))

    scr = {}
    for l in range(DEPTH):
        for n, (K, src, c0, M) in WDEFS.items():
            scr[(n, l)] = nc.dram_tensor("scr_%s_%d" % (n, l), [128, (K // 128) * M], BF16).ap()

    xT = nc.alloc_sbuf_tensor("xT", [128, KD, TP], F32)
    hT = nc.alloc_sbuf_tensor("hT", [128, KD, TP], BF16)
    wring = nc.alloc_sbuf_tensor("wring", [128, NSLOT, SLOT_ELEMS], BF16)
    hst = nc.alloc_sbuf_tensor("hst", [128, DEPTH, H * HP], F32)
    hstb = nc.alloc_sbuf_tensor("hstb", [128, H * HP], BF16)
    ctail = nc.alloc_sbuf_tensor("ctail", [128, DEPTH, 24, 3], F32)
    ident_f = nc.alloc_sbuf_tensor("ident_f", [128, 128], F32)
    ident_b = nc.alloc_sbuf_tensor("ident_b", [128, 128], BF16)
    ones_b = nc.alloc_sbuf_tensor("ones_b", [128, 128], BF16)
    ones_f = nc.alloc_sbuf_tensor("ones_f", [128, 128], F32)
    onesD_b = nc.alloc_sbuf_tensor("onesD_b", [128, 128], BF16)
    U_f = nc.alloc_sbuf_tensor("U_f", [128, 128], F32)
    SL_f = nc.alloc_sbuf_tensor("SL_f", [128, 128], F32)
    vfm = nc.alloc_sbuf_tensor("vfm", [128, DEPTH, NV], F32)
    fng = nc.alloc_sbuf_tensor("fng", [128, KD], F32)
    hv = nc.alloc_sbuf_tensor("hv", [128, DEPTH, 96], F32)
    gwT = nc.alloc_sbuf_tensor("gwT", [128, DEPTH, 8, 128], BF16)
    sctail = nc.alloc_sbuf_tensor("sctail", [128, 4, 24, 3], F32)
    small = nc.alloc_sbuf_tensor("small", [128, 64], F32)
    sq = nc.alloc_sbuf_tensor("sq", [128, 2, TP], BF16)
    rstd = nc.alloc_sbuf_tensor("rstd", [128, TP], F32)
    ARENA_B = 86016
    arena = nc.alloc_sbuf_tensor("arena", [128, ARENA_B // 4], F32)
    psb = [nc.alloc_psum_tensor("ps%d" % i, [128, 512], F32) for i in range(8)]
    psbb = [p.bitcast(BF16) for p in psb]

    def carve(off, shape, dt):
        n = int(np.prod(shape))
        nb = n * (2 if dt == BF16 else 4)
        assert off % 4 == 0 and off + nb <= ARENA_B, (off, shape)
        a = arena[:, off // 4:(off + nb + 3) // 4]
        if dt == BF16:
            a = a.bitcast(BF16)
        a = a[:, 0:n]
        if len(shape) == 2:
            return a.rearrange("p (a b) -> p a b", a=shape[0], b=shape[1])
        if len(shape) == 3:
            return a.rearrange("p (a b c) -> p a b c", a=shape[0], b=shape[1], c=shape[2])
        return a

    def dump(name, ap, dt=F32):
        if not dbg:
            return
        S.barrier()
        shp = list(ap.shape)
        o = nc.dram_tensor("dbg_" + name, shp, dt, kind="ExternalOutput").ap()
        for e in ("pe", "act", "dve", "pool"):
            if S.insts[e]:
                S._need("pool", ("e", e, len(S.insts[e]) - 1), force=True)
        S.dma("pool", "dbg_" + name, o, ap)

    ps_rr = [0]

    def new_ps(bf=False):
        i = ps_rr[0] % 6
        ps_rr[0] += 1
        return (psbb[i] if bf else psb[i]), "ps%d" % i

    slot_rr = [0]

    def load_slab(name, l, c0, w):
        K = WDEFS[name][0]
        kc = K // 128
        i = slot_rr[0] % NSLOT
        slot_rr[0] += 1
        dst = wring[:, i, 0:kc * w]
        src = scr[(name, l)][:, kc * c0:kc * (c0 + w)]
        S.dma("sp", "w%d" % i, dst, src, reads=[("scr", name, l)], writes=["w%d" % i])
        return dst.rearrange("p (k w) -> p k w", k=kc, w=w), "w%d" % i

    def linear_fm(name, l, T, act_fn, act_res, evac):
        K = WDEFS[name][0]
        kc = K // 128
        for (c0, w) in slabs_of(name):
            slab, sres = load_slab(name, l, c0, w)
            for mi in range(w // 128):
                ps, pres = new_ps()
                for k in range(kc):
                    S.op("pe", lambda e, k=k, ps=ps, mi=mi, slab=slab: e.matmul(
                        ps[:, 0:T], slab[:, k, mi * 128:(mi + 1) * 128], act_fn(k),
                        start=(k == 0), stop=(k == kc - 1)),
                        reads=[sres] + list(act_res(k)), writes=[pres])
                evac(c0 // 128 + mi, ps, pres)

    def linear_tm(name, l, Q, NT, act_fn, act_res, evac):
        K = WDEFS[name][0]
        kc = K // 128
        for (c0, w) in slabs_of(name):
            slab, sres = load_slab(name, l, c0, w)
            for j in range(NT):
                ps, pres = new_ps()
                for k in range(kc):
                    S.op("pe", lambda e, k=k, ps=ps, j=j, slab=slab: e.matmul(
                        ps[0:Q, 0:w], act_fn(k)[:, j * Q:(j + 1) * Q], slab[:, k, :],
                        start=(k == 0), stop=(k == kc - 1)),
                        reads=[sres] + list(act_res(k)), writes=[pres])
                evac(j, c0, w, ps, pres)

    def compute_rstd(T):
        ps, pres = new_ps()
        for k in range(KD):
            b = k % 2
            S.op("act", lambda e, k=k, b=b: e.activation(sq[:, b, 0:T], xT[:, k, 0:T], AF.Square),
                 reads=[("xT", k)], writes=[("sq", b)])
            S.op("pe", lambda e, k=k, b=b, ps=ps: e.matmul(ps[:, 0:T], onesD_b[:, :], sq[:, b, 0:T],
                                                            start=(k == 0), stop=(k == KD - 1)),
                 reads=[("sq", b), "onesD_b"], writes=[pres])
        S.op("act", lambda e: e.activation(rstd[:, 0:T], ps[:, 0:T], AF.Sqrt, bias=EPS, scale=1.0),
             reads=[pres], writes=["rstd"])
        S.op("dve", lambda e: e.reciprocal(rstd[:, 0:T], rstd[:, 0:T]), reads=["rstd"], writes=["rstd"])

    def rmsnorm_to_hT(l_gain_ap, T):
        compute_rstd(T)
        for k in range(KD):
            S.op("dve", lambda e, k=k: e.scalar_tensor_tensor(
                hT[:, k, 0:T], xT[:, k, 0:T], l_gain_ap[:, k:k + 1], rstd[:, 0:T], op0=ALU.mult, op1=ALU.mult),
                reads=[("xT", k), "rstd"], writes=[("hT", k)])

    def ffn(l, which, T, vcol):
        g, u, d = ("f1g", "f1u", "f1d") if which == 1 else ("f2g", "f2u", "f2d")
        hid = carve(0, (KF, TP), BF16)
        sg = carve(45056, (2, TP), BF16)
        rmsnorm_to_hT(vfm[:, l, vcol:vcol + KD], T)
        gs = slabs_of(g)
        K = D
        kc = KD
        for (c0, w) in gs:
            slab_g, rg = load_slab(g, l, c0, w)
            slab_u, ru = load_slab(u, l, c0, w)
            for mi in range(w // 128):
                m = c0 // 128 + mi
                pg, rpg = new_ps()
                pu, rpu = new_ps()
                for k in range(kc):
                    S.op("pe", lambda e, k=k, pg=pg, mi=mi, slab=slab_g: e.matmul(
                        pg[:, 0:T], slab[:, k, mi * 128:(mi + 1) * 128], hT[:, k, 0:T],
                        start=(k == 0), stop=(k == kc - 1)), reads=[rg, ("hT", k)], writes=[rpg])
                for k in range(kc):
                    S.op("pe", lambda e, k=k, pu=pu, mi=mi, slab=slab_u: e.matmul(
                        pu[:, 0:T], slab[:, k, mi * 128:(mi + 1) * 128], hT[:, k, 0:T],
                        start=(k == 0), stop=(k == kc - 1)), reads=[ru, ("hT", k)], writes=[rpu])
                b = m % 2
                S.op("act", lambda e, pg=pg, b=b: e.activation(sg[:, b, 0:T], pg[:, 0:T], AF.Silu),
                     reads=[rpg], writes=[("sg", b)])
                S.op("dve", lambda e, pu=pu, b=b, m=m: e.tensor_tensor(hid[:, m, 0:T], sg[:, b, 0:T], pu[:, 0:T], ALU.mult),
                     reads=[rpu, ("sg", b)], writes=[("hid", m)])

        def evac(m, ps, pres):
            S.op("dve", lambda e: e.scalar_tensor_tensor(xT[:, m, 0:T], ps[:, 0:T], 0.5, xT[:, m, 0:T],
                                                         op0=ALU.mult, op1=ALU.add),
                 reads=[pres, ("xT", m)], writes=[("xT", m)])
        linear_fm(d, l, T, lambda k: hid[:, k, 0:T], lambda k: [("hid", k)], evac)

    S.op("pool", lambda e: e.memset(ones_f[:, :], 1.0), writes=["ones_f"])
    S.op("pool", lambda e: e.memset(ones_b[:, :], 1.0), writes=["ones_b"])
    S.op("pool", lambda e: e.memset(onesD_b[:, :], 1.0 / D), writes=["onesD_b"])
    S.op("pool", lambda e: e.affine_select(U_f[:, :], ones_f[:, :], [[1, 128]], ALU.is_ge, 0.0, base=0,
                                           channel_multiplier=-1), reads=["ones_f"], writes=["U_f"])
    S.op("pool", lambda e: e.affine_select(SL_f[:, :], ones_f[:, :], [[-1, 128]], ALU.is_gt, 0.0, base=0,
                                           channel_multiplier=1), reads=["ones_f"], writes=["SL_f"])
    S.op("pool", lambda e: e.affine_select(ident_f[:, :], ones_f[:, :], [[1, 128]], ALU.is_equal, 0.0, base=0,
                                           channel_multiplier=-1), reads=["ones_f"], writes=["ident_f"])
    S.op("dve", lambda e: e.tensor_copy(ident_b[:, :], ident_f[:, :]), reads=["ident_f"], writes=["ident_b"])
    for l in range(DEPTH):
        S.dma("sp", "cst_v%d" % l, vfm[:, l, :], vecs[l], writes=[("vfm", l)])
        S.dma("sp", "cst_h%d" % l, hv[:, l, :], hvec[l, 0].partition_broadcast(128), writes=[("hv", l)])
    S.dma("sp", "cst_f", fng[:, :], fnorm, writes=["fng"])
    for l in range(DEPTH):
        S.op("act", lambda e, l=l: e.activation(hv[:, l, 32:64], hv[:, l, 32:64], AF.Exp), reads=[("hv", l)], writes=[("hv", l)])
        S.op("dve", lambda e, l=l: e.tensor_scalar(hv[:, l, 32:64], hv[:, l, 32:64], -1.0, None, op0=ALU.mult),
             reads=[("hv", l)], writes=[("hv", l)])
        gwT_f = carve(0, (8, 128), F32)
        S.dma("sp", "cst2", gwT_f[:, :, :], gws[l], writes=["gwT_f"])
        S.op("dve", lambda e, l=l: e.tensor_tensor(gwT[:, l, :, :], gwT_f[:, :, :],
                                                   U_f[:, :].unsqueeze(1).to_broadcast([128, 8, 128]), ALU.mult),
             reads=["gwT_f", "U_f"], writes=[("gwT", l)])

    def cast_weights(l, names):
        for n in names:
            K, src, c0, M = WDEFS[n]
            kc = K // 128
            key = "c_%s_%d" % (n, l)
            for (s0, w) in slabs_of(n):
                srcap = wsrc[src][l][:, c0 + s0:c0 + s0 + w].rearrange("(k p) w -> p k w", p=128)
                dstap = scr[(n, l)][:, kc * s0:kc * (s0 + w)].rearrange("p (k w) -> p k w", k=kc, w=w)
                tok = S.dma("pool", key, dstap, srcap)
            S.set_writer(("scr", n, l), tok)
    order = ["f1g", "f1u", "f1d", "idt", "ib", "ic", "iu", "iv", "iz", "ix", "iga", "igb", "wa", "wb", "wo",
             "f2g", "f2u", "f2d", "pg", "pp"]
    if cast_names is not None:
        order = [n for n in order if n in cast_names]
    cast_done = set()

    A_BT, A_CT, A_BTOK, A_YB, A_DT, A_DTA, A_YT = 0, 4096, 8192, 12288, 20480, 20992, 21504
    A_EC, A_DE, A_EL, A_CUMS = 37888, 38400, 38912, 39424
    AB = 39936
    A_XIN = 49152

    def bcx(ap, shape, axis):
        return ap.unsqueeze(axis).to_broadcast(shape)

    def transposes_out(src_fn, nblk, Q, dst_fn, en="dve"):
        pass

    def state_out(l, g, dst):
        i = ost["n"] % 2
        st = carve(AB + 36128 + 2048 * i, (4, 128), F32)
        sres = ("stout", i)
        ost["n"] += 1
        ps, pres = new_ps()
        for k in range(4):
            S.op("pe", lambda e, k=k, ps=ps: e.transpose(ps[:, k * 128:(k + 1) * 128],
                                                        hst[:, l, g * 512 + k * 128:g * 512 + (k + 1) * 128], ident_f[:, :]),
                 reads=[("hst", l, g), "ident_f"], writes=[pres])
        S.op("act", lambda e, ps=ps, st=st: e.copy(st[:, :, :], ps[:, :].rearrange("p (a b) -> p a b", a=4, b=128)),
             reads=[pres], writes=[sres])
        S.dma("pool", "stout%d" % i, dst.rearrange("(k p) n -> p k n", p=128), st[:, :, :], reads=[sres], arena_read=True)

    def state_in(l, g, src):
        i = ost["n"] % 2
        st = carve(AB + 36128 + 2048 * i, (4, 128), F32)
        sres = ("stout", i)
        key = "stout%d" % i
        ost["n"] += 1
        S.dma("sp", key + "i", st[:, :, :], src.rearrange("(k p) n -> p k n", p=128), writes=[sres], after_bar=True)
        ps, pres = new_ps()
        for k in range(4):
            S.op("pe", lambda e, k=k, ps=ps, st=st: e.transpose(ps[:, k * 128:(k + 1) * 128], st[:, k, :], ident_f[:, :]),
                 reads=[sres, "ident_f"], writes=[pres])
        S.op("dve", lambda e, ps=ps: e.tensor_copy(hst[:, l, g * 512:(g + 1) * 512], ps[:, :]),
             reads=[pres], writes=[("hst", l, g)])

    def tails_out(src_fn, dst):
        ct = carve(AB, (3072,), F32)
        for c4 in range(6):
            ps, pres = new_ps()
            for cc in range(4):
                c = c4 * 4 + cc
                S.op("pe", lambda e, c=c, cc=cc, ps=ps: e.transpose(ps[0:3, cc * 128:(cc + 1) * 128], src_fn(c), ident_f[:, :]),
                     reads=["ctail", "ident_f"], writes=[pres])
            S.op("dve", lambda e, ps=ps, c4=c4: e.tensor_copy(ct[0:3, c4 * 512:(c4 + 1) * 512], ps[0:3, :]),
                 reads=[pres], writes=["ctout"])
        S.dma("pool", "ctout", dst, ct[0:3, :], reads=["ctout"], arena_read=True)

    def mixer(l, T, Q, NT, kind, ti):
        nseg, L = (1, T) if kind == "p" else (4, 64)
        BT = carve(A_BT, (4, TP), BF16)
        CT = carve(A_CT, (4, TP), BF16)
        Btok = carve(A_BTOK, (4, 512), BF16)
        yb = carve(A_YB, (8, TP), BF16)
        dtt = carve(A_DT, (4, 32), F32)
        dta = carve(A_DTA, (4, 32), F32)
        yT = carve(A_YT, (16, TP), BF16)
        ec = carve(A_EC, (4, 32), F32)
        de = carve(A_DE, (4, 32), F32)
        eL = carve(A_EL, (4, 32), F32)
        cums = carve(A_CUMS, (4, 32), F32)
        rmsnorm_to_hT(vfm[:, l, V_MXN:V_MXN + KD], T)
        hact = lambda k: hT[:, k, 0:T]
        hres = lambda k: [("hT", k)]

        uT = carve(AB, (8, TP), BF16)
        vg = carve(AB + 8192, (4, 1024), F32)
        vn = carve(AB + 24576, (4, 1024), BF16)
        lngb = carve(AB + 32768, (2, 1024), F32)
        gbias = carve(AB + 40960, (8, 128), F32)
        S.dma("sp", "lng", lngb[:, 0, :], lng[l, 0].partition_broadcast(128), writes=["lng"], after_bar=True)
        S.dma("sp", "lnb", lngb[:, 1, :], lnb[l, 0].partition_broadcast(128), writes=["lnb"], after_bar=True)
        S.dma("sp", "gbs", gbias[:, :, :], gbs[l, 0].partition_broadcast(128).rearrange("p (g t) -> p g t", g=8, t=128),
              writes=["gbias"], after_bar=True)

        def ev_dt(j, c0, w, ps, pres):
            S.op("dve", lambda e: e.tensor_tensor(dtt[0:Q, j, :], ps[0:Q, 0:32], hv[0:Q, l, 0:32], ALU.add),
                 reads=[pres, ("hv", l)], writes=[("dtt", j)])
            S.op("act", lambda e: e.activation(dtt[0:Q, j, :], dtt[0:Q, j, :], AF.Exp), reads=[("dtt", j)], writes=[("dtt", j)])
            S.op("act", lambda e: e.activation(dtt[0:Q, j, :], dtt[0:Q, j, :], AF.Ln, bias=1.0), reads=[("dtt", j)], writes=[("dtt", j)])
            S.op("dve", lambda e: e.tensor_tensor(dta[0:Q, j, :], dtt[0:Q, j, :], hv[0:Q, l, 32:64], ALU.mult),
                 reads=[("dtt", j), ("hv", l)], writes=[("dta", j)])
        linear_tm("idt", l, Q, NT, hact, hres, ev_dt)

        def ev_u(m, ps, pres):
            S.op("act", lambda e: e.activation(uT[:, m, 0:T], ps[:, 0:T], AF.Gelu_apprx_tanh), reads=[pres], writes=[("uT", m)])
        linear_fm("iu", l, T, hact, hres, ev_u)

        def ev_v(j, c0, w, ps, pres):
            S.op("act", lambda e: e.activation(vg[0:Q, j, c0:c0 + w], ps[0:Q, 0:w], AF.Gelu_apprx_tanh),
                 reads=[pres], writes=[("vg", j, c0)])
        linear_tm("iv", l, Q, NT, hact, hres, ev_v)
        vgres = lambda j: [("vg", j, c0) for (c0, w) in slabs_of("iv")]
        for j in range(NT):
            st6 = small[0:Q, 0:12]
            S.op("dve", lambda e: e.bn_stats(small[0:Q, 0:6], vg[0:Q, j, 0:512]), reads=vgres(j), writes=["st6a"])
            S.op("dve", lambda e: e.bn_stats(small[0:Q, 6:12], vg[0:Q, j, 512:1024]), reads=vgres(j), writes=["st6b"])
            S.op("dve", lambda e: e.bn_aggr(small[0:Q, 12:14], st6), reads=["st6a", "st6b"], writes=["mv"])
            S.op("act", lambda e: e.activation(small[0:Q, 14:15], small[0:Q, 13:14], AF.Sqrt, bias=EPS, scale=1.0),
                 reads=["mv"], writes=["lnr"])
            S.op("dve", lambda e: e.reciprocal(small[0:Q, 14:15], small[0:Q, 14:15]), reads=["lnr"], writes=["lnr"])
            S.op("dve", lambda e: e.tensor_scalar(vg[0:Q, j, :], vg[0:Q, j, :], small[0:Q, 12:13], small[0:Q, 14:15],
                                                  op0=ALU.subtract, op1=ALU.mult),
                 reads=vgres(j) + ["mv", "lnr"], writes=[("vgn", j)])
            S.op("pool", lambda e: e.tensor_tensor(vg[0:Q, j, :], vg[0:Q, j, :], lngb[0:Q, 0, :], ALU.mult),
                 reads=[("vgn", j), "lng"], writes=[("vgn", j)])
            S.op("pool", lambda e: e.tensor_tensor(vg[0:Q, j, :], vg[0:Q, j, :], lngb[0:Q, 1, :], ALU.add),
                 reads=[("vgn", j), "lnb"], writes=[("vgn", j)])
            S.op("act", lambda e: e.copy(vn[0:Q, j, :], vg[0:Q, j, :]), reads=[("vgn", j)], writes=[("vn", j)])
            if kind == "s":
                S.dma("pool", "gv", gv[l, j * Q:(j + 1) * Q, :], vg[0:Q, j, :], reads=[("vgn", j)], arena_read=True)
        for g8 in range(8):
            ps, pres = new_ps()
            for j in range(NT):
                S.op("pe", lambda e, j=j, ps=ps: e.matmul(ps[:, j * Q:(j + 1) * Q], vn[0:Q, j, g8 * 128:(g8 + 1) * 128],
                                                           gwT[0:Q, l, g8, 0:Q], start=True, stop=True),
                     reads=[("vn", j), ("gwT", l)], writes=[pres])
            S.op("dve", lambda e, ps=ps: e.tensor_tensor(
                ps_tmp[:, 0:T].rearrange("p (j q) -> p j q", j=NT, q=Q),
                ps[:, 0:T].rearrange("p (j q) -> p j q", j=NT, q=Q),
                bcx(gbias[:, g8, 0:Q], [128, NT, Q], 1), ALU.add), reads=[pres, "gbias"], writes=["ps_tmp"])
            S.op("dve", lambda e: e.tensor_tensor(yb[:, g8, 0:T], ps_tmp[:, 0:T], uT[:, g8, 0:T], ALU.mult),
                 reads=["ps_tmp", ("uT", g8)], writes=[("yb", g8)])

        dump("dtt", dtt[:, :, :])
        dump("vn", vn[:, :, :], BF16)
        dump("yb", yb[:, :, :], BF16)
        S.barrier()
        zs = carve(AB, (4, 512), BF16)
        xs_tok = carve(AB + 4096, (4, 512), BF16)
        RAWSZ = 2064
        raws = [carve(AB + 8192 + i * RAWSZ, (nseg, 3 + L), F32) for i in range(2)]
        accs = [carve(AB + 12320 + i * 2048, (nseg, L), F32) for i in range(2)]
        xc = carve(AB + 16416, (2, TP), BF16)
        Wh = carve(AB + 18464, (8, 128), F32)
        E = carve(AB + 22560, (8, 128), BF16)
        Mm = carve(AB + 24608, (8, 128), BF16)
        CBm = carve(AB + 26656, (128,), BF16)
        xdt = carve(AB + 26912, (512,), BF16)
        xde = carve(AB + 27936, (512,), BF16)
        yo = carve(AB + 28960, (512,), F32)
        yy = carve(AB + 31008, (512,), F32)
        tmpd = carve(AB + 33056, (512,), F32)
        yn = carve(AB + 35104, (512,), BF16)
        cvi = [0]

        def conv_chunk(c, ps, pres, dst, dres):
            i = cvi[0] % 2
            cvi[0] += 1
            raw, acc = raws[i], accs[i]
            rr, ra = ("raw", i), ("acc", i)
            if kind == "p":
                S.op("pool", lambda e: e.tensor_copy(raw[:, 0, 0:3], ctail[:, l, c, :]), reads=[("ctail", l, c)], writes=[rr])
            else:
                S.op("pool", lambda e: e.tensor_copy(raw[:, :, 0:3], sctail[:, :, c, :]), reads=[("sctail", l)], writes=[rr])
            S.op("act", lambda e: e.copy(raw[:, :, 3:3 + L], ps[:, 0:T].rearrange("p (s q) -> p s q", s=nseg, q=L)),
                 reads=[pres], writes=[rr])
            if kind == "p":
                S.op("pool", lambda e: e.tensor_copy(ctail[:, l, c, :], raw[:, 0, L:L + 3]), reads=[rr], writes=[("ctail", l, c)])
            else:
                S.op("pool", lambda e: e.tensor_copy(sctout[:, :, c, :], raw[:, :, L:L + 3]), reads=[rr], writes=[("sctout", l)])
            wc = lambda k: vfm[:, l, V_CW + k * 24 + c:V_CW + k * 24 + c + 1]
            S.op("dve", lambda e: e.tensor_scalar(acc[:, :, :], raw[:, :, 0:L], wc(0), vfm[:, l, V_CB + c:V_CB + c + 1],
                                                  op0=ALU.mult, op1=ALU.add), reads=[rr], writes=[ra])
            for k in range(1, 4):
                S.op("dve", lambda e, k=k: e.scalar_tensor_tensor(acc[:, :, :], raw[:, :, k:k + L], wc(k), acc[:, :, :],
                                                                  op0=ALU.mult, op1=ALU.add), reads=[rr, ra], writes=[ra])
            S.op("act", lambda e: e.activation(dst.rearrange("p (s q) -> p s q", s=nseg, q=L), acc[:, :, :], AF.Silu),
                 reads=[ra], writes=[dres])

        def ev_b(m, ps, pres):
            conv_chunk(16 + m, ps, pres, BT[:, m, 0:T], ("BT", m))
            for j in range(NT):
                pt, ptr = new_ps(bf=True)
                S.op("pe", lambda e: e.transpose(pt[0:Q, 0:128], BT[:, m, j * Q:(j + 1) * Q], ident_b[:, :]),
                     reads=[("BT", m), "ident_b"], writes=[ptr])
                S.op("dve", lambda e: e.tensor_copy(Btok[0:Q, j, m * 128:(m + 1) * 128], pt[0:Q, 0:128]),
                     reads=[ptr], writes=[("Btok", j, m)])
        linear_fm("ib", l, T, hact, hres, ev_b)

        def ev_c(m, ps, pres):
            conv_chunk(20 + m, ps, pres, CT[:, m, 0:T], ("CT", m))
        linear_fm("ic", l, T, hact, hres, ev_c)

        for j in range(NT):
            pc, rc = new_ps()
            S.op("pe", lambda e: e.matmul(pc[0:Q, 0:32], U_f[0:Q, 0:Q], dta[0:Q, j, :], start=True, stop=True),
                 reads=[("dta", j), "U_f"], writes=[rc])
            pl, rl = new_ps()
            S.op("pe", lambda e: e.matmul(pl[:, 0:32], ones_f[0:Q, :], dta[0:Q, j, :], start=True, stop=True),
                 reads=[("dta", j), "ones_f"], writes=[rl])
            S.op("act", lambda e: e.activation(ec[0:Q, j, :], pc[0:Q, 0:32], AF.Exp), reads=[rc], writes=[("ec", j)])
            S.op("dve", lambda e: e.tensor_copy(cums[0:Q, j, :], pc[0:Q, 0:32]), reads=[rc], writes=[("cums", j)])
            S.op("dve", lambda e: e.tensor_tensor(de[0:Q, j, :], pl[0:Q, 0:32], cums[0:Q, j, :], ALU.subtract),
                 reads=[rl, ("cums", j)], writes=[("de", j)])
            S.op("act", lambda e: e.activation(de[0:Q, j, :], de[0:Q, j, :], AF.Exp), reads=[("de", j)], writes=[("de", j)])
            S.op("act", lambda e: e.activation(eL[:, j, :], pl[:, 0:32], AF.Exp), reads=[rl], writes=[("eL", j)])

        hpb = 512 // Q
        for g in range(G):
            gs = slice(g * 8, (g + 1) * 8)
            def ev_z(j, c0, w, ps, pres):
                S.op("act", lambda e: e.activation(zs[0:Q, j, c0 - g * 512:c0 - g * 512 + w], ps[0:Q, 0:w], AF.Silu),
                     reads=[pres], writes=[("zs", j, c0)])
            K = D
            for (c0, w) in [sl for sl in slabs_of("iz") if g * 512 <= sl[0] < (g + 1) * 512]:
                slab, sres = load_slab("iz", l, c0, w)
                for j in range(NT):
                    ps, pres = new_ps()
                    for k in range(KD):
                        S.op("pe", lambda e, k=k, ps=ps, j=j, slab=slab: e.matmul(
                            ps[0:Q, 0:w], hT[:, k, j * Q:(j + 1) * Q], slab[:, k, :], start=(k == 0), stop=(k == KD - 1)),
                            reads=[sres, ("hT", k)], writes=[pres])
                    ev_z(j, c0, w, ps, pres)
            for (c0, w) in [sl for sl in slabs_of("ix") if g * 512 <= sl[0] < (g + 1) * 512]:
                slab, sres = load_slab("ix", l, c0, w)
                for mi in range(w // 128):
                    m = c0 // 128 + mi
                    cc = m - g * 4
                    ps, pres = new_ps()
                    for k in range(KD):
                        S.op("pe", lambda e, k=k, ps=ps, mi=mi, slab=slab: e.matmul(
                            ps[:, 0:T], slab[:, k, mi * 128:(mi + 1) * 128], hT[:, k, 0:T], start=(k == 0), stop=(k == KD - 1)),
                            reads=[sres, ("hT", k)], writes=[pres])
                    conv_chunk(m, ps, pres, xc[:, m % 2, 0:T], ("xc", m % 2))
                    for j in range(NT):
                        pt, ptr = new_ps(bf=True)
                        S.op("pe", lambda e, j=j, pt=pt, m=m: e.transpose(pt[0:Q, 0:128], xc[:, m % 2, j * Q:(j + 1) * Q], ident_b[:, :]),
                             reads=[("xc", m % 2), "ident_b"], writes=[ptr])
                        S.op("dve", lambda e, j=j, pt=pt, cc=cc: e.tensor_copy(xs_tok[0:Q, j, cc * 128:(cc + 1) * 128], pt[0:Q, 0:128]),
                             reads=[ptr], writes=[("xs_tok", j, cc)])
            zres = lambda j: [("zs", j, c0) for (c0, w) in slabs_of("iz") if g * 512 <= c0 < (g + 1) * 512]
            xres = lambda j: [("xs_tok", j, cc) for cc in range(4)]
            if kind == "p":
                S.op("act", lambda e: e.copy(hstb[:, g * 512:(g + 1) * 512], hst[:, l, g * 512:(g + 1) * 512]),
                     reads=[("hst", l, g)], writes=[("hstb", g)])
            xdes = [xde, carve(AB + 40224, (512,), BF16)]
            psy = {}

            def ssd_front(j):
                jq = slice(j * Q, (j + 1) * Q)
                xde = xdes[j % 2]
                S.op("pool", lambda e: e.tensor_tensor(Wh[0:Q, :, 0:Q], bcx(SL_f[0:Q, 0:Q], [Q, 8, Q], 1),
                                                       bcx(dta[0:Q, j, gs], [Q, 8, Q], 2), ALU.mult),
                     reads=["SL_f", ("dta", j)], writes=["Wh"])
                for hb in range(8 // hpb):
                    ps_s, r_s = new_ps()
                    for hh in range(hpb):
                        h8 = hb * hpb + hh
                        S.op("pe", lambda e, hh=hh, h8=h8, ps_s=ps_s: e.matmul(ps_s[0:Q, hh * Q:(hh + 1) * Q], Wh[0:Q, h8, 0:Q], U_f[0:Q, 0:Q],
                                                                                 start=True, stop=True),
                             reads=["Wh", "U_f"], writes=[r_s])
                    S.op("act", lambda e, hb=hb, ps_s=ps_s: e.activation(
                        E[0:Q, hb * hpb:(hb + 1) * hpb, 0:Q], ps_s[0:Q, 0:hpb * Q].rearrange("p (h q) -> p h q", h=hpb, q=Q), AF.Exp),
                        reads=[r_s], writes=[("E", hb)])
                eres = [("E", hb) for hb in range(8 // hpb)]
                ps_cb, r_cb = new_ps()
                S.op("pe", lambda e: e.matmul(ps_cb[0:Q, 0:Q], BT[:, g, jq], CT[:, g, jq], start=True, stop=True),
                     reads=[("BT", g), ("CT", g)], writes=[r_cb])
                S.op("dve", lambda e: e.tensor_tensor(CBm[0:Q, 0:Q], ps_cb[0:Q, 0:Q], U_f[0:Q, 0:Q], ALU.mult),
                     reads=[r_cb, "U_f"], writes=["CBm"])
                S.op("dve", lambda e: e.tensor_tensor(Mm[0:Q, :, 0:Q], E[0:Q, :, 0:Q], bcx(CBm[0:Q, 0:Q], [Q, 8, Q], 1), ALU.mult),
                     reads=eres + ["CBm"], writes=["Mm"])
                S.op("pool", lambda e: e.tensor_tensor(xdt[0:Q, :].rearrange("p (h q) -> p h q", h=8, q=64),
                                                       xs_tok[0:Q, j, :].rearrange("p (h q) -> p h q", h=8, q=64),
                                                       bcx(dtt[0:Q, j, gs], [Q, 8, 64], 2), ALU.mult),
                     reads=xres(j) + [("dtt", j)], writes=["xdt"])
                S.op("pool", lambda e: e.tensor_tensor(xde[0:Q, :].rearrange("p (h q) -> p h q", h=8, q=64),
                                                       xdt[0:Q, :].rearrange("p (h q) -> p h q", h=8, q=64),
                                                       bcx(de[0:Q, j, gs], [Q, 8, 64], 2), ALU.mult),
                     reads=["xdt", ("de", j)], writes=[("xde", j % 2)])
                ps_y, r_y = psb[6 + j % 2], "ps%d" % (6 + j % 2)
                for h8 in range(8):
                    S.op("pe", lambda e, h8=h8: e.matmul(ps_y[0:Q, h8 * 64:(h8 + 1) * 64], Mm[0:Q, h8, 0:Q], xdt[0:Q, h8 * 64:(h8 + 1) * 64],
                                                          start=True, stop=True), reads=["Mm", "xdt"], writes=[r_y])
                psy[j] = (ps_y, r_y)

            def ssd_back(j):
                jq = slice(j * Q, (j + 1) * Q)
                ps_y, r_y = psy[j]
                xde = xdes[j % 2]
                if kind == "s":
                    state_in(l, g, ss[l, j, g * 512:(g + 1) * 512, :])
                    S.op("act", lambda e: e.copy(hstb[:, g * 512:(g + 1) * 512], hst[:, l, g * 512:(g + 1) * 512]),
                         reads=[("hst", l, g)], writes=[("hstb", g)])
                ps_o, r_o = new_ps()
                S.op("pe", lambda e: e.matmul(ps_o[0:Q, 0:512], CT[:, g, jq], hstb[:, g * 512:(g + 1) * 512], start=True, stop=True),
                     reads=[("CT", g), ("hstb", g)], writes=[r_o])
                S.op("dve", lambda e: e.tensor_tensor(yo[0:Q, :].rearrange("p (h q) -> p h q", h=8, q=64),
                                                      ps_o[0:Q, 0:512].rearrange("p (h q) -> p h q", h=8, q=64),
                                                      bcx(ec[0:Q, j, gs], [Q, 8, 64], 2), ALU.mult),
                     reads=[r_o, ("ec", j)], writes=["yo"])
                S.op("dve", lambda e: e.tensor_tensor(yy[0:Q, :], ps_y[0:Q, 0:512], yo[0:Q, :], ALU.add),
                     reads=[r_y, "yo"], writes=["yy"])
                S.op("pool", lambda e: e.tensor_tensor(tmpd[0:Q, :].rearrange("p (h q) -> p h q", h=8, q=64),
                                                       xs_tok[0:Q, j, :].rearrange("p (h q) -> p h q", h=8, q=64),
                                                       bcx(hv[0:Q, l, 64 + g * 8:64 + (g + 1) * 8], [Q, 8, 64], 2), ALU.mult),
                     reads=xres(j) + [("hv", l)], writes=["tmpd"])
                S.op("dve", lambda e: e.tensor_tensor(yy[0:Q, :], yy[0:Q, :], tmpd[0:Q, :], ALU.add), reads=["yy", "tmpd"], writes=["yy"])
                S.op("dve", lambda e: e.tensor_tensor(yy[0:Q, :], yy[0:Q, :], zs[0:Q, j, :], ALU.mult), reads=["yy"] + zres(j), writes=["yy"])
                S.op("pool", lambda e: e.memset(small[0:Q, 16:17], 0.0), writes=["ssq"])
                S.op("act", lambda e: e.activation(tmpd[0:Q, :], yy[0:Q, :], AF.Square, accum_out=small[0:Q, 16:17]),
                     reads=["yy"], writes=["tmpd", "ssq"])
                S.op("act", lambda e: e.activation(small[0:Q, 17:18], small[0:Q, 16:17], AF.Sqrt, bias=EPS, scale=1.0 / 512),
                     reads=["ssq"], writes=["grs"])
                S.op("dve", lambda e: e.reciprocal(small[0:Q, 17:18], small[0:Q, 17:18]), reads=["grs"], writes=["grs"])
                S.op("dve", lambda e: e.tensor_scalar(yn[0:Q, :], yy[0:Q, :], small[0:Q, 17:18], None, op0=ALU.mult),
                     reads=["yy", "grs"], writes=["yn"])
                pt, ptr = new_ps(bf=True)
                for cc in range(4):
                    S.op("pe", lambda e, cc=cc: e.transpose(pt[:, cc * Q:(cc + 1) * Q], yn[0:Q, cc * 128:(cc + 1) * 128], ident_b[0:Q, 0:Q]),
                         reads=["yn", "ident_b"], writes=[ptr])
                S.op("dve", lambda e: e.tensor_tensor(yT[:, g * 4:(g + 1) * 4, jq], pt[:, 0:4 * Q].rearrange("p (c q) -> p c q", c=4, q=Q),
                                                      bcx(vfm[:, l, V_SSN + g * 4:V_SSN + (g + 1) * 4], [128, 4, Q], 2), ALU.mult),
                     reads=[ptr, ("vfm", l)], writes=[("yT", g, j)])
                ps_h, r_h = new_ps()
                S.op("pe", lambda e: e.matmul(ps_h[:, 0:512], Btok[0:Q, j, g * 128:(g + 1) * 128], xde[0:Q, :], start=True, stop=True),
                     reads=[("Btok", j, g), ("xde", j % 2)], writes=[r_h])
                hv3 = hst[:, l, g * 512:(g + 1) * 512].rearrange("p (h q) -> p h q", h=8, q=64)
                S.op("dve", lambda e: e.tensor_tensor(hv3, hv3, bcx(eL[:, j, gs], [128, 8, 64], 2), ALU.mult),
                     reads=[("hst", l, g), ("eL", j)], writes=[("hst", l, g)])
                S.op("dve", lambda e: e.tensor_tensor(hst[:, l, g * 512:(g + 1) * 512], hst[:, l, g * 512:(g + 1) * 512], ps_h[:, 0:512], ALU.add),
                     reads=[("hst", l, g), r_h], writes=[("hst", l, g)])
                if kind == "s":
                    state_out(l, g, ssm_s[l, j, g * 512:(g + 1) * 512, :])
                elif j < NT - 1:
                    S.op("act", lambda e: e.copy(hstb[:, g * 512:(g + 1) * 512], hst[:, l, g * 512:(g + 1) * 512]),
                         reads=[("hst", l, g)], writes=[("hstb", g)])

            ssd_front(0)
            for j in range(NT):
                if j + 1 < NT:
                    ssd_front(j + 1)
                ssd_back(j)

        dump("yT", yT[:, :, :], BF16)
        dump("hst", hst[:, :, :])
        S.barrier()
        P1 = carve(AB, (16, TP), F32)
        sgm = carve(AB + 32768, (2, TP), BF16)
        tmpm = carve(AB + 34816, (2, TP), F32)
        merged = yT
        yTres = lambda k: [("yT", k // 4, j) for j in range(NT)]
        for (c0, w) in slabs_of("wa"):
            slab_a, ra = load_slab("wa", l, c0, w)
            slab_g, rg = load_slab("iga", l, c0, w)
            for mi in range(w // 128):
                n = c0 // 128 + mi
                pa, rpa = new_ps()
                pg, rpg = new_ps()
                for k in range(KD):
                    S.op("pe", lambda e, k=k: e.matmul(pa[:, 0:T], slab_a[:, k, mi * 128:(mi + 1) * 128], yT[:, k, 0:T],
                                                        start=(k == 0), stop=(k == KD - 1)), reads=[ra] + yTres(k), writes=[rpa])
                for k in range(KD):
                    S.op("pe", lambda e, k=k: e.matmul(pg[:, 0:T], slab_g[:, k, mi * 128:(mi + 1) * 128], hT[:, k, 0:T],
                                                        start=(k == 0), stop=(k == KD - 1)), reads=[rg, ("hT", k)], writes=[rpg])
                b = n % 2
                S.op("act", lambda e: e.activation(sgm[:, b, 0:T], pg[:, 0:T], AF.Sigmoid), reads=[rpg], writes=[("sgm", b)])
                S.op("dve", lambda e: e.tensor_tensor(P1[:, n, 0:T], sgm[:, b, 0:T], pa[:, 0:T], ALU.mult),
                     reads=[("sgm", b), rpa], writes=[("P1", n)])
        S.barrier()
        for (c0, w) in slabs_of("wb"):
            slab_b, rb = load_slab("wb", l, c0, w)
            slab_g, rg = load_slab("igb", l, c0, w)
            for mi in range(w // 128):
                n = c0 // 128 + mi
                pb, rpb = new_ps()
                pg, rpg = new_ps()
                for k in range(8):
                    S.op("pe", lambda e, k=k: e.matmul(pb[:, 0:T], slab_b[:, k, mi * 128:(mi + 1) * 128], yb[:, k, 0:T],
                                                        start=(k == 0), stop=(k == 7)), reads=[rb, ("yb", k)], writes=[rpb])
                for k in range(KD):
                    S.op("pe", lambda e, k=k: e.matmul(pg[:, 0:T], slab_g[:, k, mi * 128:(mi + 1) * 128], hT[:, k, 0:T],
                                                        start=(k == 0), stop=(k == KD - 1)), reads=[rg, ("hT", k)], writes=[rpg])
                b = n % 2
                S.op("act", lambda e: e.activation(sgm[:, b, 0:T], pg[:, 0:T], AF.Sigmoid), reads=[rpg], writes=[("sgm", b)])
                S.op("dve", lambda e: e.tensor_tensor(tmpm[:, b, 0:T], sgm[:, b, 0:T], pb[:, 0:T], ALU.mult),
                     reads=[("sgm", b), rpb], writes=[("tmpm", b)])
                S.op("pool", lambda e: e.tensor_tensor(merged[:, n, 0:T], tmpm[:, b, 0:T], P1[:, n, 0:T], ALU.add),
                     reads=[("tmpm", b), ("P1", n)], writes=[("mrg", n)])

        dump("merged", merged[:, :, :], BF16)

        def ev_o(m, ps, pres):
            S.op("dve", lambda e: e.tensor_tensor(xT[:, m, 0:T], xT[:, m, 0:T], ps[:, 0:T], ALU.add),
                 reads=[pres, ("xT", m)], writes=[("xT", m)])
        linear_fm("wo", l, T, lambda k: merged[:, k, 0:T], lambda k: [("mrg", k)], ev_o)

    def ple(l, T, Q, NT, psrc):
        pst = [carve(i * 1024, (256,), F32) for i in range(2)]
        pT = carve(2048, (2, TP), BF16)
        ppw = carve(4096, (4, 2, 512), BF16)
        sgm = carve(12288, (2, TP), BF16)
        tmp = carve(14336, (2, TP), F32)
        S.dma("sp", "ppw", ppw[:, :, :, :].rearrange("p s k w -> p (s k w)"), scr[("pp", l)][:, :], reads=[("scr", "pp", l)], writes=["ppw"], after_bar=True)
        for j in range(NT):
            S.dma("sp", "pst%d" % (j % 2), pst[j % 2][0:Q, :], psrc[j * Q:(j + 1) * Q, :], writes=[("pst", j % 2)], after_bar=True)
            ps, pres = new_ps()
            for k in range(2):
                S.op("pe", lambda e, k=k, ps=ps: e.transpose(ps[:, k * Q:(k + 1) * Q], pst[j % 2][0:Q, k * 128:(k + 1) * 128], ident_f[0:Q, 0:Q]),
                     reads=[("pst", j % 2), "ident_f"], writes=[pres])
            S.op("dve", lambda e, ps=ps: e.tensor_copy(pT[:, :, j * Q:(j + 1) * Q], ps[:, 0:2 * Q].rearrange("p (k q) -> p k q", k=2, q=Q)),
                 reads=[pres], writes=[("pT", j)])
        rmsnorm_to_hT(vfm[:, l, V_PLN:V_PLN + KD], T)
        pTres = [("pT", j) for j in range(NT)]

        def ev(m, ps, pres):
            pp_, rpp = new_ps()
            for k in range(2):
                S.op("pe", lambda e, k=k: e.matmul(pp_[:, 0:T], ppw[:, m // 4, k, (m % 4) * 128:(m % 4 + 1) * 128], pT[:, k, 0:T], start=(k == 0), stop=(k == 1)),
                     reads=["ppw"] + pTres, writes=[rpp])
            b = m % 2
            S.op("act", lambda e: e.activation(sgm[:, b, 0:T], ps[:, 0:T], AF.Sigmoid), reads=[pres], writes=[("sgm", b)])
            S.op("dve", lambda e: e.tensor_tensor(tmp[:, b, 0:T], sgm[:, b, 0:T], pp_[:, 0:T], ALU.mult),
                 reads=[("sgm", b), rpp], writes=[("ptmp", b)])
            S.op("pool", lambda e: e.tensor_tensor(xT[:, m, 0:T], xT[:, m, 0:T], tmp[:, b, 0:T], ALU.add),
                 reads=[("ptmp", b), ("xT", m)], writes=[("xT", m)])
        linear_fm("pg", l, T, lambda k: hT[:, k, 0:T], lambda k: [("hT", k)], ev)

    tiles = [("p", i) for i in range(n_ptiles)]
    if with_sample:
        tiles.append(("s", 0))
    ost = {"n": 0}
    ps_tmp = nc.alloc_sbuf_tensor("ps_tmp", [128, TP], F32)
    sctout = nc.alloc_sbuf_tensor("sctout", [128, 4, 24, 3], F32)
    S.op("pool", lambda e: e.memset(hst[:, :, :], 0.0), writes=[("hst", l, g) for l in range(DEPTH) for g in range(G)])
    S.op("pool", lambda e: e.memset(ctail[:, :, :, :], 0.0), writes=[("ctail", l, c) for l in range(DEPTH) for c in range(24)])

    for (kind, ti) in tiles:
        if kind == "p":
            T, Q, NT = TP, 128, TP // 128
            xsrc = xp[ti * TP:(ti + 1) * TP, :]
            ydst = yp[ti * TP:(ti + 1) * TP, :]
        else:
            T, Q, NT = 256, 64, 4
            xsrc = xs
            ydst = ys
        bar = S.barrier()
        for j in range(NT):
            xin = carve(A_XIN + (j % 2) * 8192, (KD, 128), F32)
            S.dma("sp", "xin%d" % (j % 2), xin[0:Q], xsrc[j * Q:(j + 1) * Q, :].rearrange("t (k c) -> t k c", k=KD),
                  writes=[("xio", j % 2)], after_bar=True)
            for k4 in range(KD // 4):
                ps, pres = new_ps()
                for kk in range(4):
                    k = k4 * 4 + kk
                    S.op("pe", lambda e, ps=ps, kk=kk, k=k, xin=xin: e.transpose(
                        ps[:, kk * 128:kk * 128 + Q], xin[0:Q, k, :], ident_f[0:Q, 0:Q]),
                        reads=[("xio", j % 2), "ident_f"], writes=[pres])
                en = "act" if k4 % 2 else "dve"
                fn = (lambda e, ps=ps, k4=k4, j=j: e.copy(
                    xT[:, k4 * 4:(k4 + 1) * 4, j * Q:(j + 1) * Q], ps[:, :].rearrange("p (a b) -> p a b", a=4, b=128)[:, :, 0:Q])) \
                    if en == "act" else (lambda e, ps=ps, k4=k4, j=j: e.tensor_copy(
                        xT[:, k4 * 4:(k4 + 1) * 4, j * Q:(j + 1) * Q], ps[:, :].rearrange("p (a b) -> p a b", a=4, b=128)[:, :, 0:Q]))
                S.op(en, fn, reads=[pres], writes=[("xT", k4 * 4 + i) for i in range(4)])

        stopped = False
        for l in range(layers):
            if l not in cast_done:
                cast_weights(l, order)
                cast_done.add(l)
            if kind == "s":
                for sq_ in range(4):
                    cs = carve(0, (3072,), F32)
                    S.dma("sp", "scin", cs[0:3, :], sc[l, sq_], writes=["scin"], after_bar=True)
                    for c4 in range(6):
                        ps, pres = new_ps()
                        for cc in range(4):
                            c = c4 * 4 + cc
                            S.op("pe", lambda e, c=c, cc=cc, ps=ps, cs=cs: e.transpose(ps[:, cc * 3:(cc + 1) * 3], cs[0:3, c * 128:(c + 1) * 128], ident_f[0:3, 0:3]),
                                 reads=["scin", "ident_f"], writes=[pres])
                        S.op("dve", lambda e, ps=ps, c4=c4, sq_=sq_: e.tensor_copy(sctail[:, sq_, c4 * 4:(c4 + 1) * 4, :],
                                                                                    ps[:, 0:12].rearrange("p (c k) -> p c k", c=4, k=3)),
                             reads=[pres], writes=[("sctail", l)])
                S.barrier()
            ffn(l, 1, T, V_F1N)
            if stop_after == (l, "ffn1"):
                stopped = True
                break
            S.barrier()
            mixer(l, T, Q, NT, kind, ti)
            if l + 1 < layers and (l + 1) not in cast_done:
                cast_weights(l + 1, order)
                cast_done.add(l + 1)
            if stop_after == (l, "mixer"):
                stopped = True
                break
            S.barrier()
            ffn(l, 2, T, V_F2N)
            if stop_after == (l, "ffn2"):
                stopped = True
                break
            S.barrier()
            ple(l, T, Q, NT, (ppr[l, ti * TP:(ti + 1) * TP, :] if kind == "p" else psm[l]))
            if stop_after == (l, "ple"):
                stopped = True
                break
            S.barrier()
            if kind == "s":
                for sq_ in range(4):
                    tails_out(lambda c, sq_=sq_: sctout[:, sq_, c, :], conv_s[l, sq_])
                S.barrier()

        bar = S.barrier()
        if not stopped:
            compute_rstd(T)
            for k in range(KD):
                S.op("dve", lambda e, k=k: e.scalar_tensor_tensor(
                    xT[:, k, 0:T], xT[:, k, 0:T], fng[:, k:k + 1], rstd[:, 0:T], op0=ALU.mult, op1=ALU.mult),
                    reads=[("xT", k), "rstd", "fng"], writes=[("xT", k)])
        for j in range(NT):
            yo_ = carve(A_XIN + (j % 2) * 8192, (KD, 128), F32)
            for k4 in range(KD // 4):
                ps, pres = new_ps()
                for kk in range(4):
                    k = k4 * 4 + kk
                    S.op("pe", lambda e, ps=ps, kk=kk, k=k, j=j: e.transpose(
                        ps[0:Q, kk * 128:(kk + 1) * 128], xT[:, k, j * Q:(j + 1) * Q], ident_f[:, :]),
                        reads=[("xT", k), "ident_f"], writes=[pres])
                S.op("dve", lambda e, ps=ps, k4=k4, yo_=yo_: e.tensor_copy(
                    yo_[0:Q, k4 * 4:(k4 + 1) * 4, :], ps[0:Q, :].rearrange("p (a b) -> p a b", a=4, b=128)),
                    reads=[pres], writes=[("xio", j % 2)])
            S.dma("pool", "yo%d" % (j % 2), ydst[j * Q:(j + 1) * Q, :].rearrange("t (k c) -> t k c", k=KD), yo_[0:Q],
                  reads=[("xio", j % 2)], arena_read=True)
        if kind == "p" and ti == n_ptiles - 1 and not stopped:
            S.barrier()
            for l in range(layers):
                for g in range(G):
                    state_out(l, g, ssm_p[l, g * 512:(g + 1) * 512, :])
                tails_out(lambda c, l=l: ctail[:, l, c, :], conv_p[l])
            S.barrier()
    S.finish()
    return nc, S


def _fm(v, k):
    return np.ascontiguousarray(np.asarray(v, np.float32).reshape(k, 128).T)


ACTIVE = [0, 1, 4, 5]


def make_in_maps(inp, n_cores=8):
    vecs = np.zeros((DEPTH, 128, NV), np.float32)
    for l in range(DEPTH):
        vecs[l, :, V_F1N:V_F1N + 16] = _fm(inp["ffn1_norm"][l], 16)
        vecs[l, :, V_MXN:V_MXN + 16] = _fm(inp["mix_norm"][l], 16)
        vecs[l, :, V_F2N:V_F2N + 16] = _fm(inp["ffn2_norm"][l], 16)
        vecs[l, :, V_PLN:V_PLN + 16] = _fm(inp["ple_norm"][l], 16)
        vecs[l, :, V_SSN:V_SSN + 16] = _fm(inp["ssd_norm"][l], 16)
        vecs[l, :, V_CB:V_CB + 24] = _fm(inp["conv_b"][l], 24)
        for k in range(4):
            vecs[l, :, V_CW + k * 24:V_CW + (k + 1) * 24] = _fm(inp["conv_w"][l][k], 24)
    hvec = np.concatenate([np.asarray(inp["dt_bias"]), np.asarray(inp["a_log"]), np.asarray(inp["d_skip"])],
                          axis=1).astype(np.float32).reshape(DEPTH, 1, 96)
    shared = {n: np.ascontiguousarray(np.asarray(inp[n], np.float32)) for n in WSRC}
    shared["vecs_fm"] = vecs
    shared["fnorm_fm"] = _fm(inp["final_norm"], 16)
    shared["hvec"] = hvec
    shared["gm_ln_g"] = np.asarray(inp["gm_ln_g"], np.float32).reshape(DEPTH, 1, GMW)
    shared["gm_ln_b"] = np.asarray(inp["gm_ln_b"], np.float32).reshape(DEPTH, 1, GMW)
    shared["gm_wsT"] = np.ascontiguousarray(np.transpose(np.asarray(inp["gm_ws"], np.float32), (0, 3, 1, 2)))
    shared["gm_bs"] = np.asarray(inp["gm_bs"], np.float32).reshape(DEPTH, 1, 8 * 128)
    maps = []
    zero_shared = None
    for c in range(n_cores):
        if n_cores == 8 and c not in ACTIVE:
            if zero_shared is None:
                zero_shared = {k: np.zeros_like(v) for k, v in shared.items()}
                zero_shared.update({"xp": np.zeros((SEQ, D), np.float32), "ppr": np.zeros((DEPTH, SEQ, PLE), np.float32),
                                    "xs": np.zeros((256, D), np.float32), "psm": np.zeros((DEPTH, 256, PLE), np.float32),
                                    "sc": np.zeros((DEPTH, 4, 3, XBC), np.float32),
                                    "ss": np.zeros((DEPTH, 4, H * HP, NS), np.float32)})
            maps.append(zero_shared)
            continue
        b = ACTIVE.index(c) if n_cores == 8 else c % 4
        m = dict(shared)
        m["xp"] = np.ascontiguousarray(np.asarray(inp["x_prompt"][b], np.float32))
        m["ppr"] = np.ascontiguousarray(np.asarray(inp["p_prompt"][:, b], np.float32))
        m["xs"] = np.ascontiguousarray(np.asarray(inp["x_sample"][4 * b:4 * b + 4], np.float32).reshape(256, D))
        m["psm"] = np.ascontiguousarray(np.asarray(inp["p_sample"][:, 4 * b:4 * b + 4], np.float32).reshape(DEPTH, 256, PLE))
        m["sc"] = np.ascontiguousarray(np.asarray(inp["state_conv"][:, 4 * b:4 * b + 4], np.float32))
        m["ss"] = np.ascontiguousarray(np.asarray(inp["state_ssm"][:, 4 * b:4 * b + 4], np.float32).reshape(DEPTH, 4, H * HP, NS))
        maps.append(m)
    return maps


_CACHE = {}


def kernel(**inputs):
    if "nc" not in _CACHE:
        _CACHE["nc"] = build()[0]
    nc = _CACHE["nc"]
    maps = make_in_maps(inputs, 8)
    allres = run_bass_kernel_spmd(nc, maps, core_ids=list(range(8))).results
    res = [allres[c] for c in ACTIVE]
    B, DB = 4, 16
    y_prompt = np.stack([res[b]["yp"] for b in range(B)])
    y_sample = np.concatenate([res[b]["ys"].reshape(4, 64, D) for b in range(B)])
    ssm_prompt = np.stack([res[b]["ssm_p"].reshape(DEPTH, H, HP, NS) for b in range(B)], axis=1)
    conv_prompt = np.stack([res[b]["conv_p"] for b in range(B)], axis=1)
    ssm_sample = np.concatenate([res[b]["ssm_s"].reshape(DEPTH, 4, H, HP, NS) for b in range(B)], axis=1)
    conv_sample = np.concatenate([res[b]["conv_s"] for b in range(B)], axis=1)
    gv = np.concatenate([res[b]["gv"].reshape(DEPTH, 4, 64, GMW) for b in range(B)], axis=1)
    f = lambda a: np.ascontiguousarray(a, dtype=np.float32)
    return (f(y_prompt), f(y_sample), f(ssm_prompt), f(conv_prompt), f(ssm_sample), f(conv_sample), f(gv))
```

```python
import bisect
import numpy as np
import concourse.bass as bass
import concourse.mybir as mybir
from concourse.bass_utils import run_bass_kernel_spmd

F32 = mybir.dt.float32
BF16 = mybir.dt.bfloat16
AF = mybir.ActivationFunctionType
ALU = mybir.AluOpType

D = 2048
KD = 16
DFF = 5632
KF = 44
H = 32
HP = 64
G = 4
NS = 128
XBC = 3072
GMW = 1024
PLE = 256
DEPTH = 2
SEQ = 8192
EPS = 1e-6
SLOT_ELEMS = 6144
NSLOT = 3


class Sched:
    def __init__(self, nc):
        self.nc = nc
        self.eng = {"pe": nc.tensor, "act": nc.scalar, "dve": nc.vector,
                    "pool": nc.gpsimd, "sp": nc.sync}
        self.sem = {}
        self.cnt = {}
        self.insts = {}
        self.sig_idx = {}
        self.sig_val = {}
        self.waited = {}
        for k in self.eng:
            self.sem[k] = nc.semaphore("s_" + k).__enter__()
            self.cnt[k] = 0
            self.insts[k] = []
            self.sig_idx[k] = []
            self.sig_val[k] = []
            self.waited[k] = {}
        self.dsem = {}
        self.dcnt = {}
        self.last_w = {}
        self.readers = {}
        self.nwaits = 0
        self.last_bar = []
        self.last_bar_d = {}
        self.arena_rd = set()

    def _signal(self, en, idx):
        si = self.sig_idx[en]
        j = bisect.bisect_left(si, idx)
        if j < len(si):
            return self.sig_val[en][j]
        inst = self.insts[en][idx]
        inst.then_inc(self.sem[en], 1)
        self.cnt[en] += 1
        si.append(idx)
        self.sig_val[en].append(self.cnt[en])
        return self.cnt[en]

    def _need(self, en, tok, raw=True, force=False):
        if tok is None:
            return
        if tok[0] == "e":
            _, src, idx = tok
            if src == en and not force:
                if not raw or en == "pe" or idx != len(self.insts[en]) - 1:
                    return
            val = self._signal(src, idx)
            key = ("e", src)
            sem = self.sem[src]
        else:
            _, dk, val = tok
            key = ("d", dk)
            sem = self.dsem[dk]
        if self.waited[en].get(key, 0) >= val:
            return
        self.eng[en].wait_ge(sem, val)
        self.waited[en][key] = val
        self.nwaits += 1

    def _deps(self, en, reads, writes, force=False):
        for r in reads:
            self._need(en, self.last_w.get(r), raw=True, force=force)
        for w in writes:
            self._need(en, self.last_w.get(w), raw=False, force=force)
            rd = self.readers.get(w)
            if rd:
                for t in rd.values():
                    self._need(en, t, raw=False, force=force)

    def op(self, en, fn, reads=(), writes=()):
        self._deps(en, reads, writes)
        inst = fn(self.eng[en])
        idx = len(self.insts[en])
        self.insts[en].append(inst)
        tok = ("e", en, idx)
        for r in reads:
            self.readers.setdefault(r, {})[en] = tok
        for w in writes:
            self.last_w[w] = tok
            self.readers[w] = {}
        return tok

    def set_writer(self, res, tok):
        self.last_w[res] = tok
        self.readers[res] = {}

    def dma(self, q, key, out, in_, reads=(), writes=(), extra=(), after_bar=False, arena_read=False, **kw):
        if key not in self.dsem:
            self.dsem[key] = self.nc.semaphore("d_" + key).__enter__()
            self.dcnt[key] = 0
        self._deps(q, reads, writes, force=True)
        for t in extra:
            self._need(q, t, force=True)
        if after_bar:
            for t in self.last_bar:
                self._need(q, t, force=True)
            for k, c in self.last_bar_d.items():
                self._need(q, ("d", k, c))
        if arena_read:
            self.arena_rd.add(key)
        self.eng[q].dma_start(out=out, in_=in_, **kw).then_inc(self.dsem[key], 16)
        self.dcnt[key] += 16
        tok = ("d", key, self.dcnt[key])
        for r in reads:
            self.readers.setdefault(r, {})["d:" + key] = tok
        for w in writes:
            self.last_w[w] = tok
            self.readers[w] = {}
        return tok

    def barrier(self, engines=("pe", "act", "dve", "pool")):
        toks = []
        for e in engines:
            if self.insts[e]:
                toks.append(("e", e, len(self.insts[e]) - 1))
        for e in engines:
            for t in toks:
                if t[1] != e:
                    self._need(e, t, force=True)
            for k in self.arena_rd:
                self._need(e, ("d", k, self.dcnt[k]))
        self.last_bar = toks
        self.last_bar_d = {k: self.dcnt[k] for k in self.arena_rd}
        return toks

    def finish(self):
        for e in ("pe", "act", "dve", "pool"):
            if self.insts[e]:
                self._need("sp", ("e", e, len(self.insts[e]) - 1), force=True)
        for k, c in self.dcnt.items():
            self._need("sp", ("d", k, c))


IN_Z0 = 0
IN_X0 = 2048
IN_B0 = 4096
IN_C0 = 4608
IN_DT0 = 5120
IN_U0 = 5152
IN_V0 = 6176
IN_GA0 = 7200
IN_GB0 = 9248
IN_COLS = 11296

WDEFS = {
    "f1g": (D, "ffn1_w_gate", 0, DFF), "f1u": (D, "ffn1_w_up", 0, DFF), "f1d": (DFF, "ffn1_w_down", 0, D),
    "f2g": (D, "ffn2_w_gate", 0, DFF), "f2u": (D, "ffn2_w_up", 0, DFF), "f2d": (DFF, "ffn2_w_down", 0, D),
    "iz": (D, "w_in", IN_Z0, 2048), "ix": (D, "w_in", IN_X0, 2048), "ib": (D, "w_in", IN_B0, 512),
    "ic": (D, "w_in", IN_C0, 512), "idt": (D, "w_in", IN_DT0, 32), "iu": (D, "w_in", IN_U0, 1024),
    "iv": (D, "w_in", IN_V0, 1024), "iga": (D, "w_in", IN_GA0, 2048), "igb": (D, "w_in", IN_GB0, 2048),
    "wa": (D, "w_branch_ssd", 0, D), "wb": (GMW, "w_branch_gmlp", 0, D), "wo": (D, "w_out", 0, D),
    "pg": (D, "ple_w_gate", 0, D), "pp": (PLE, "ple_w_proj", 0, D),
}
WSRC = ["ffn1_w_gate", "ffn1_w_up", "ffn1_w_down", "w_in", "w_branch_ssd", "w_branch_gmlp", "w_out",
        "ffn2_w_gate", "ffn2_w_up", "ffn2_w_down", "ple_w_gate", "ple_w_proj"]
WSRC_SHAPE = {"ffn1_w_gate": (D, DFF), "ffn1_w_up": (D, DFF), "ffn1_w_down": (DFF, D), "w_in": (D, IN_COLS),
              "w_branch_ssd": (D, D), "w_branch_gmlp": (GMW, D), "w_out": (D, D),
              "ffn2_w_gate": (D, DFF), "ffn2_w_up": (D, DFF), "ffn2_w_down": (DFF, D),
              "ple_w_gate": (D, D), "ple_w_proj": (PLE, D)}


WIDTH = {"iz": 256, "ix": 256, "wb": 256, "igb": 256}


def slabs_of(name):
    K, _, _, M = WDEFS[name]
    kc = K // 128
    w = WIDTH.get(name, min((SLOT_ELEMS // kc) // 128 * 128, 512))
    out = []
    c = 0
    while c < M:
        ww = min(w, M - c)
        out.append((c, ww))
        c += ww
    return out


V_F1N, V_MXN, V_F2N, V_PLN, V_SSN, V_CB, V_CW = 0, 16, 32, 48, 64, 80, 104
NV = 104 + 96


def build(n_ptiles=16, with_sample=True, stop_after=None, TP=512, layers=DEPTH, cast_names=None, dbg=False):
    nc = bass.Bass("TRN2", target_bir_lowering=False)
    S = Sched(nc)
    dram_in = {}

    def din(name, shape, dt=F32):
        dram_in[name] = nc.dram_tensor(name, list(shape), dt, kind="ExternalInput").ap()
        return dram_in[name]

    def dout(name, shape):
        return nc.dram_tensor(name, list(shape), F32, kind="ExternalOutput").ap()

    xp = din("xp", (SEQ, D))
    ppr = din("ppr", (DEPTH, SEQ, PLE))
    xs = din("xs", (256, D))
    psm = din("psm", (DEPTH, 256, PLE))
    sc = din("sc", (DEPTH, 4, 3, XBC))
    ss = din("ss", (DEPTH, 4, H * HP, NS))
    wsrc = {n: din(n, (DEPTH,) + WSRC_SHAPE[n]) for n in WSRC}
    vecs = din("vecs_fm", (DEPTH, 128, NV))
    fnorm = din("fnorm_fm", (128, KD))
    hvec = din("hvec", (DEPTH, 1, 96))
    lng = din("gm_ln_g", (DEPTH, 1, GMW))
    lnb = din("gm_ln_b", (DEPTH, 1, GMW))
    gws = din("gm_wsT", (DEPTH, 128, 8, 128))
    gbs = din("gm_bs", (DEPTH, 1, 8 * 128))

    yp = dout("yp", (SEQ, D))
    ys = dout("ys", (256, D))
    ssm_p = dout("ssm_p", (DEPTH, H * HP, NS))
    conv_p = dout("conv_p", (DEPTH, 3, XBC))
    ssm_s = dout("ssm_s", (DEPTH, 4, H * HP, NS))
    conv_s = dout("conv_s", (DEPTH, 4, 3, XBC))
    gv = dout("gv", (DEPTH, 256, GMW))

    scr = {}
    for l in range(DEPTH):
        for n, (K, src, c0, M) in WDEFS.items():
            scr[(n, l)] = nc.dram_tensor("scr_%s_%d" % (n, l), [128, (K // 128) * M], BF16).ap()

    xT = nc.alloc_sbuf_tensor("xT", [128, KD, TP], F32)
    hT = nc.alloc_sbuf_tensor("hT", [128, KD, TP], BF16)
    wring = nc.alloc_sbuf_tensor("wring", [128, NSLOT, SLOT_ELEMS], BF16)
    hst = nc.alloc_sbuf_tensor("hst", [128, DEPTH, H * HP], F32)
    hstb = nc.alloc_sbuf_tensor("hstb", [128, H * HP], BF16)
    ctail = nc.alloc_sbuf_tensor("ctail", [128, DEPTH, 24, 3], F32)
    ident_f = nc.alloc_sbuf_tensor("ident_f", [128, 128], F32)
    ident_b = nc.alloc_sbuf_tensor("ident_b", [128, 128], BF16)
    ones_b = nc.alloc_sbuf_tensor("ones_b", [128, 128], BF16)
    ones_f = nc.alloc_sbuf_tensor("ones_f", [128, 128], F32)
    onesD_b = nc.alloc_sbuf_tensor("onesD_b", [128, 128], BF16)
    U_f = nc.alloc_sbuf_tensor("U_f", [128, 128], F32)
    SL_f = nc.alloc_sbuf_tensor("SL_f", [128, 128], F32)
    vfm = nc.alloc_sbuf_tensor("vfm", [128, DEPTH, NV], F32)
    fng = nc.alloc_sbuf_tensor("fng", [128, KD], F32)
    hv = nc.alloc_sbuf_tensor("hv", [128, DEPTH, 96], F32)
    gwT = nc.alloc_sbuf_tensor("gwT", [128, DEPTH, 8, 128], BF16)
    sctail = nc.alloc_sbuf_tensor("sctail", [128, 4, 24, 3], F32)
    small = nc.alloc_sbuf_tensor("small", [128, 64], F32)
    sq = nc.alloc_sbuf_tensor("sq", [128, 2, TP], BF16)
    rstd = nc.alloc_sbuf_tensor("rstd", [128, TP], F32)
    ARENA_B = 86016
    arena = nc.alloc_sbuf_tensor("arena", [128, ARENA_B // 4], F32)
    psb = [nc.alloc_psum_tensor("ps%d" % i, [128, 512], F32) for i in range(8)]
    psbb = [p.bitcast(BF16) for p in psb]

    def carve(off, shape, dt):
        n = int(np.prod(shape))
        nb = n * (2 if dt == BF16 else 4)
        assert off % 4 == 0 and off + nb <= ARENA_B, (off, shape)
        a = arena[:, off // 4:(off + nb + 3) // 4]
        if dt == BF16:
            a = a.bitcast(BF16)
        a = a[:, 0:n]
        if len(shape) == 2:
            return a.rearrange("p (a b) -> p a b", a=shape[0], b=shape[1])
        if len(shape) == 3:
            return a.rearrange("p (a b c) -> p a b c", a=shape[0], b=shape[1], c=shape[2])
        return a

    def dump(name, ap, dt=F32):
        if not dbg:
            return
        S.barrier()
        shp = list(ap.shape)
        o = nc.dram_tensor("dbg_" + name, shp, dt, kind="ExternalOutput").ap()
        for e in ("pe", "act", "dve", "pool"):
            if S.insts[e]:
                S._need("pool", ("e", e, len(S.insts[e]) - 1), force=True)
        S.dma("pool", "dbg_" + name, o, ap)

    ps_rr = [0]

    def new_ps(bf=False):
        i = ps_rr[0] % 8
        ps_rr[0] += 1
        return (psbb[i] if bf else psb[i]), "ps%d" % i

    slot_rr = [0]

    def load_slab(name, l, c0, w):
        K = WDEFS[name][0]
        kc = K // 128
        i = slot_rr[0] % NSLOT
        slot_rr[0] += 1
        dst = wring[:, i, 0:kc * w]
        src = scr[(name, l)][:, kc * c0:kc * (c0 + w)]
        S.dma("sp", "w%d" % i, dst, src, reads=[("scr", name, l)], writes=["w%d" % i])
        return dst.rearrange("p (k w) -> p k w", k=kc, w=w), "w%d" % i

    def linear_fm(name, l, T, act_fn, act_res, evac):
        K = WDEFS[name][0]
        kc = K // 128
        for (c0, w) in slabs_of(name):
            slab, sres = load_slab(name, l, c0, w)
            for mi in range(w // 128):
                ps, pres = new_ps()
                for k in range(kc):
                    S.op("pe", lambda e, k=k, ps=ps, mi=mi, slab=slab: e.matmul(
                        ps[:, 0:T], slab[:, k, mi * 128:(mi + 1) * 128], act_fn(k),
                        start=(k == 0), stop=(k == kc - 1)),
                        reads=[sres] + list(act_res(k)), writes=[pres])
                evac(c0 // 128 + mi, ps, pres)

    def linear_tm(name, l, Q, NT, act_fn, act_res, evac):
        K = WDEFS[name][0]
        kc = K // 128
        for (c0, w) in slabs_of(name):
            slab, sres = load_slab(name, l, c0, w)
            for j in range(NT):
                ps, pres = new_ps()
                for k in range(kc):
                    S.op("pe", lambda e, k=k, ps=ps, j=j, slab=slab: e.matmul(
                        ps[0:Q, 0:w], act_fn(k)[:, j * Q:(j + 1) * Q], slab[:, k, :],
                        start=(k == 0), stop=(k == kc - 1)),
                        reads=[sres] + list(act_res(k)), writes=[pres])
                evac(j, c0, w, ps, pres)

    def compute_rstd(T):
        ps, pres = new_ps()
        for k in range(KD):
            b = k % 2
            S.op("act", lambda e, k=k, b=b: e.activation(sq[:, b, 0:T], xT[:, k, 0:T], AF.Square),
                 reads=[("xT", k)], writes=[("sq", b)])
            S.op("pe", lambda e, k=k, b=b, ps=ps: e.matmul(ps[:, 0:T], onesD_b[:, :], sq[:, b, 0:T],
                                                            start=(k == 0), stop=(k == KD - 1)),
                 reads=[("sq", b), "onesD_b"], writes=[pres])
        S.op("act", lambda e: e.activation(rstd[:, 0:T], ps[:, 0:T], AF.Sqrt, bias=EPS, scale=1.0),
             reads=[pres], writes=["rstd"])
        S.op("dve", lambda e: e.reciprocal(rstd[:, 0:T], rstd[:, 0:T]), reads=["rstd"], writes=["rstd"])

    def rmsnorm_to_hT(l_gain_ap, T):
        compute_rstd(T)
        for k in range(KD):
            S.op("dve", lambda e, k=k: e.scalar_tensor_tensor(
                hT[:, k, 0:T], xT[:, k, 0:T], l_gain_ap[:, k:k + 1], rstd[:, 0:T], op0=ALU.mult, op1=ALU.mult),
                reads=[("xT", k), "rstd"], writes=[("hT", k)])

    def ffn(l, which, T, vcol):
        g, u, d = ("f1g", "f1u", "f1d") if which == 1 else ("f2g", "f2u", "f2d")
        hid = carve(0, (KF, TP), BF16)
        sg = carve(45056, (2, TP), BF16)
        rmsnorm_to_hT(vfm[:, l, vcol:vcol + KD], T)
        gs = slabs_of(g)
        K = D
        kc = KD
        for (c0, w) in gs:
            slab_g, rg = load_slab(g, l, c0, w)
            slab_u, ru = load_slab(u, l, c0, w)
            for mi in range(w // 128):
                m = c0 // 128 + mi
                pg, rpg = new_ps()
                pu, rpu = new_ps()
                for k in range(kc):
                    S.op("pe", lambda e, k=k, pg=pg, mi=mi, slab=slab_g: e.matmul(
                        pg[:, 0:T], slab[:, k, mi * 128:(mi + 1) * 128], hT[:, k, 0:T],
                        start=(k == 0), stop=(k == kc - 1)), reads=[rg, ("hT", k)], writes=[rpg])
                for k in range(kc):
                    S.op("pe", lambda e, k=k, pu=pu, mi=mi, slab=slab_u: e.matmul(
                        pu[:, 0:T], slab[:, k, mi * 128:(mi + 1) * 128], hT[:, k, 0:T],
                        start=(k == 0), stop=(k == kc - 1)), reads=[ru, ("hT", k)], writes=[rpu])
                b = m % 2
                S.op("act", lambda e, pg=pg, b=b: e.activation(sg[:, b, 0:T], pg[:, 0:T], AF.Silu),
                     reads=[rpg], writes=[("sg", b)])
                S.op("dve", lambda e, pu=pu, b=b, m=m: e.tensor_tensor(hid[:, m, 0:T], sg[:, b, 0:T], pu[:, 0:T], ALU.mult),
                     reads=[rpu, ("sg", b)], writes=[("hid", m)])

        def evac(m, ps, pres):
            S.op("dve", lambda e: e.scalar_tensor_tensor(xT[:, m, 0:T], ps[:, 0:T], 0.5, xT[:, m, 0:T],
                                                         op0=ALU.mult, op1=ALU.add),
                 reads=[pres, ("xT", m)], writes=[("xT", m)])
        linear_fm(d, l, T, lambda k: hid[:, k, 0:T], lambda k: [("hid", k)], evac)

    S.op("pool", lambda e: e.memset(ones_f[:, :], 1.0), writes=["ones_f"])
    S.op("pool", lambda e: e.memset(ones_b[:, :], 1.0), writes=["ones_b"])
    S.op("pool", lambda e: e.memset(onesD_b[:, :], 1.0 / D), writes=["onesD_b"])
    S.op("pool", lambda e: e.affine_select(U_f[:, :], ones_f[:, :], [[1, 128]], ALU.is_ge, 0.0, base=0,
                                           channel_multiplier=-1), reads=["ones_f"], writes=["U_f"])
    S.op("pool", lambda e: e.affine_select(SL_f[:, :], ones_f[:, :], [[-1, 128]], ALU.is_gt, 0.0, base=0,
                                           channel_multiplier=1), reads=["ones_f"], writes=["SL_f"])
    S.op("pool", lambda e: e.affine_select(ident_f[:, :], ones_f[:, :], [[1, 128]], ALU.is_equal, 0.0, base=0,
                                           channel_multiplier=-1), reads=["ones_f"], writes=["ident_f"])
    S.op("dve", lambda e: e.tensor_copy(ident_b[:, :], ident_f[:, :]), reads=["ident_f"], writes=["ident_b"])
    for l in range(DEPTH):
        S.dma("sp", "cst_v%d" % l, vfm[:, l, :], vecs[l], writes=[("vfm", l)])
        S.dma("sp", "cst_h%d" % l, hv[:, l, :], hvec[l, 0].partition_broadcast(128), writes=[("hv", l)])
    S.dma("sp", "cst_f", fng[:, :], fnorm, writes=["fng"])
    for l in range(DEPTH):
        S.op("act", lambda e, l=l: e.activation(hv[:, l, 32:64], hv[:, l, 32:64], AF.Exp), reads=[("hv", l)], writes=[("hv", l)])
        S.op("dve", lambda e, l=l: e.tensor_scalar(hv[:, l, 32:64], hv[:, l, 32:64], -1.0, None, op0=ALU.mult),
             reads=[("hv", l)], writes=[("hv", l)])
        gwT_f = carve(0, (8, 128), F32)
        S.dma("sp", "cst2", gwT_f[:, :, :], gws[l], writes=["gwT_f"])
        S.op("dve", lambda e, l=l: e.tensor_tensor(gwT[:, l, :, :], gwT_f[:, :, :],
                                                   U_f[:, :].unsqueeze(1).to_broadcast([128, 8, 128]), ALU.mult),
             reads=["gwT_f", "U_f"], writes=[("gwT", l)])

    def cast_weights(l, names):
        for n in names:
            K, src, c0, M = WDEFS[n]
            kc = K // 128
            key = "c_%s_%d" % (n, l)
            for (s0, w) in slabs_of(n):
                srcap = wsrc[src][l][:, c0 + s0:c0 + s0 + w].rearrange("(k p) w -> p k w", p=128)
                dstap = scr[(n, l)][:, kc * s0:kc * (s0 + w)].rearrange("p (k w) -> p k w", k=kc, w=w)
                tok = S.dma("pool", key, dstap, srcap)
            S.set_writer(("scr", n, l), tok)
    order = ["f1g", "f1u", "f1d", "idt", "ib", "ic", "iu", "iv", "iz", "ix", "iga", "igb", "wa", "wb", "wo",
             "f2g", "f2u", "f2d", "pg", "pp"]
    if cast_names is not None:
        order = [n for n in order if n in cast_names]
    cast_done = set()

    A_BT, A_CT, A_BTOK, A_YB, A_DT, A_DTA, A_YT = 0, 4096, 8192, 12288, 20480, 20992, 21504
    A_EC, A_DE, A_EL, A_CUMS = 37888, 38400, 38912, 39424
    AB = 39936
    A_XIN = 49152

    def bcx(ap, shape, axis):
        return ap.unsqueeze(axis).to_broadcast(shape)

    def transposes_out(src_fn, nblk, Q, dst_fn, en="dve"):
        pass

    def state_out(l, g, dst):
        i = ost["n"] % 2
        st = carve(AB + 36128 + 2048 * i, (4, 128), F32)
        sres = ("stout", i)
        ost["n"] += 1
        ps, pres = new_ps()
        for k in range(4):
            S.op("pe", lambda e, k=k, ps=ps: e.transpose(ps[:, k * 128:(k + 1) * 128],
                                                        hst[:, l, g * 512 + k * 128:g * 512 + (k + 1) * 128], ident_f[:, :]),
                 reads=[("hst", l, g), "ident_f"], writes=[pres])
        S.op("act", lambda e, ps=ps, st=st: e.copy(st[:, :, :], ps[:, :].rearrange("p (a b) -> p a b", a=4, b=128)),
             reads=[pres], writes=[sres])
        S.dma("pool", "stout%d" % i, dst.rearrange("(k p) n -> p k n", p=128), st[:, :, :], reads=[sres], arena_read=True)

    def state_in(l, g, src):
        i = ost["n"] % 2
        st = carve(AB + 36128 + 2048 * i, (4, 128), F32)
        sres = ("stout", i)
        key = "stout%d" % i
        ost["n"] += 1
        S.dma("sp", key + "i", st[:, :, :], src.rearrange("(k p) n -> p k n", p=128), writes=[sres], after_bar=True)
        ps, pres = new_ps()
        for k in range(4):
            S.op("pe", lambda e, k=k, ps=ps, st=st: e.transpose(ps[:, k * 128:(k + 1) * 128], st[:, k, :], ident_f[:, :]),
                 reads=[sres, "ident_f"], writes=[pres])
        S.op("dve", lambda e, ps=ps: e.tensor_copy(hst[:, l, g * 512:(g + 1) * 512], ps[:, :]),
             reads=[pres], writes=[("hst", l, g)])

    def tails_out(src_fn, dst):
        ct = carve(AB, (3072,), F32)
        for c4 in range(6):
            ps, pres = new_ps()
            for cc in range(4):
                c = c4 * 4 + cc
                S.op("pe", lambda e, c=c, cc=cc, ps=ps: e.transpose(ps[0:3, cc * 128:(cc + 1) * 128], src_fn(c), ident_f[:, :]),
                     reads=["ctail", "ident_f"], writes=[pres])
            S.op("dve", lambda e, ps=ps, c4=c4: e.tensor_copy(ct[0:3, c4 * 512:(c4 + 1) * 512], ps[0:3, :]),
                 reads=[pres], writes=["ctout"])
        S.dma("pool", "ctout", dst, ct[0:3, :], reads=["ctout"], arena_read=True)

    def mixer(l, T, Q, NT, kind, ti):
        nseg, L = (1, T) if kind == "p" else (4, 64)
        BT = carve(A_BT, (4, TP), BF16)
        CT = carve(A_CT, (4, TP), BF16)
        Btok = carve(A_BTOK, (4, 512), BF16)
        yb = carve(A_YB, (8, TP), BF16)
        dtt = carve(A_DT, (4, 32), F32)
        dta = carve(A_DTA, (4, 32), F32)
        yT = carve(A_YT, (16, TP), BF16)
        ec = carve(A_EC, (4, 32), F32)
        de = carve(A_DE, (4, 32), F32)
        eL = carve(A_EL, (4, 32), F32)
        cums = carve(A_CUMS, (4, 32), F32)
        rmsnorm_to_hT(vfm[:, l, V_MXN:V_MXN + KD], T)
        hact = lambda k: hT[:, k, 0:T]
        hres = lambda k: [("hT", k)]

        uT = carve(AB, (8, TP), BF16)
        vg = carve(AB + 8192, (4, 1024), F32)
        vn = carve(AB + 24576, (4, 1024), BF16)
        lngb = carve(AB + 32768, (2, 1024), F32)
        gbias = carve(AB + 40960, (8, 128), F32)
        S.dma("sp", "lng", lngb[:, 0, :], lng[l, 0].partition_broadcast(128), writes=["lng"], after_bar=True)
        S.dma("sp", "lnb", lngb[:, 1, :], lnb[l, 0].partition_broadcast(128), writes=["lnb"], after_bar=True)
        S.dma("sp", "gbs", gbias[:, :, :], gbs[l, 0].partition_broadcast(128).rearrange("p (g t) -> p g t", g=8, t=128),
              writes=["gbias"], after_bar=True)

        def ev_dt(j, c0, w, ps, pres):
            S.op("dve", lambda e: e.tensor_tensor(dtt[0:Q, j, :], ps[0:Q, 0:32], hv[0:Q, l, 0:32], ALU.add),
                 reads=[pres, ("hv", l)], writes=[("dtt", j)])
            S.op("act", lambda e: e.activation(dtt[0:Q, j, :], dtt[0:Q, j, :], AF.Exp), reads=[("dtt", j)], writes=[("dtt", j)])
            S.op("act", lambda e: e.activation(dtt[0:Q, j, :], dtt[0:Q, j, :], AF.Ln, bias=1.0), reads=[("dtt", j)], writes=[("dtt", j)])
            S.op("dve", lambda e: e.tensor_tensor(dta[0:Q, j, :], dtt[0:Q, j, :], hv[0:Q, l, 32:64], ALU.mult),
                 reads=[("dtt", j), ("hv", l)], writes=[("dta", j)])
        linear_tm("idt", l, Q, NT, hact, hres, ev_dt)

        def ev_u(m, ps, pres):
            S.op("act", lambda e: e.activation(uT[:, m, 0:T], ps[:, 0:T], AF.Gelu_apprx_tanh), reads=[pres], writes=[("uT", m)])
        linear_fm("iu", l, T, hact, hres, ev_u)

        def ev_v(j, c0, w, ps, pres):
            S.op("act", lambda e: e.activation(vg[0:Q, j, c0:c0 + w], ps[0:Q, 0:w], AF.Gelu_apprx_tanh),
                 reads=[pres], writes=[("vg", j, c0)])
        linear_tm("iv", l, Q, NT, hact, hres, ev_v)
        vgres = lambda j: [("vg", j, c0) for (c0, w) in slabs_of("iv")]
        lnops = []
        for j in range(NT):
            o = 16 * j
            ops = [
                ("dve", (lambda e, j=j, o=o: e.bn_stats(small[0:Q, o:o + 6], vg[0:Q, j, 0:512])), vgres(j), [("st6a", j)]),
                ("dve", (lambda e, j=j, o=o: e.bn_stats(small[0:Q, o + 6:o + 12], vg[0:Q, j, 512:1024])), vgres(j), [("st6b", j)]),
                ("dve", (lambda e, j=j, o=o: e.bn_aggr(small[0:Q, o + 12:o + 14], small[0:Q, o:o + 12])), [("st6a", j), ("st6b", j)], [("mv", j)]),
                ("act", (lambda e, j=j, o=o: e.activation(small[0:Q, o + 14:o + 15], small[0:Q, o + 13:o + 14], AF.Sqrt, bias=EPS, scale=1.0)),
                 [("mv", j)], [("lnr", j)]),
                ("dve", (lambda e, j=j, o=o: e.reciprocal(small[0:Q, o + 14:o + 15], small[0:Q, o + 14:o + 15])), [("lnr", j)], [("lnr", j)]),
                ("dve", (lambda e, j=j, o=o: e.tensor_scalar(vg[0:Q, j, :], vg[0:Q, j, :], small[0:Q, o + 12:o + 13], small[0:Q, o + 14:o + 15],
                                                            op0=ALU.subtract, op1=ALU.mult)),
                 vgres(j) + [("mv", j), ("lnr", j)], [("vgn", j)]),
                ("pool", (lambda e, j=j: e.tensor_tensor(vg[0:Q, j, :], vg[0:Q, j, :], lngb[0:Q, 0, :], ALU.mult)), [("vgn", j), "lng"], [("vgn", j)]),
                ("pool", (lambda e, j=j: e.tensor_tensor(vg[0:Q, j, :], vg[0:Q, j, :], lngb[0:Q, 1, :], ALU.add)), [("vgn", j), "lnb"], [("vgn", j)]),
                ("act", (lambda e, j=j: e.copy(vn[0:Q, j, :], vg[0:Q, j, :])), [("vgn", j)], [("vn", j)]),
            ]
            lnops.append(ops)
        for st in range(len(lnops[0])):
            for j in range(NT):
                en, fn, rd, wr = lnops[j][st]
                S.op(en, fn, reads=rd, writes=wr)
        if kind == "s":
            for j in range(NT):
                S.dma("pool", "gv", gv[l, j * Q:(j + 1) * Q, :], vg[0:Q, j, :], reads=[("vgn", j)], arena_read=True)
        for g8 in range(8):
            ps, pres = new_ps()
            for j in range(NT):
                S.op("pe", lambda e, j=j, ps=ps: e.matmul(ps[:, j * Q:(j + 1) * Q], vn[0:Q, j, g8 * 128:(g8 + 1) * 128],
                                                           gwT[0:Q, l, g8, 0:Q], start=True, stop=True),
                     reads=[("vn", j), ("gwT", l)], writes=[pres])
            S.op("dve", lambda e, ps=ps: e.tensor_tensor(
                ps_tmp[:, 0:T].rearrange("p (j q) -> p j q", j=NT, q=Q),
                ps[:, 0:T].rearrange("p (j q) -> p j q", j=NT, q=Q),
                bcx(gbias[:, g8, 0:Q], [128, NT, Q], 1), ALU.add), reads=[pres, "gbias"], writes=["ps_tmp"])
            S.op("dve", lambda e: e.tensor_tensor(yb[:, g8, 0:T], ps_tmp[:, 0:T], uT[:, g8, 0:T], ALU.mult),
                 reads=["ps_tmp", ("uT", g8)], writes=[("yb", g8)])

        dump("dtt", dtt[:, :, :])
        dump("vn", vn[:, :, :], BF16)
        dump("yb", yb[:, :, :], BF16)
        S.barrier()
        zs = carve(AB, (4, 512), BF16)
        xs_tok = carve(AB + 4096, (4, 512), BF16)
        RAWSZ = 2064
        raws = [carve(AB + 8192 + i * RAWSZ, (nseg, 3 + L), F32) for i in range(2)]
        accs = [carve(AB + 12320 + i * 2048, (nseg, L), F32) for i in range(2)]
        xc = carve(AB + 16416, (2, TP), BF16)
        Wh = carve(AB + 18464, (8, 128), F32)
        E = carve(AB + 22560, (8, 128), BF16)
        Mm = carve(AB + 24608, (8, 128), BF16)
        CBm = carve(AB + 26656, (128,), BF16)
        xdt = carve(AB + 26912, (512,), BF16)
        xde = carve(AB + 27936, (512,), BF16)
        yo = carve(AB + 28960, (512,), F32)
        yy = carve(AB + 31008, (512,), F32)
        tmpd = carve(AB + 33056, (512,), F32)
        yn = carve(AB + 35104, (512,), BF16)
        cvi = [0]

        def conv_chunk(c, ps, pres, dst, dres):
            i = cvi[0] % 2
            cvi[0] += 1
            raw, acc = raws[i], accs[i]
            rr, ra = ("raw", i), ("acc", i)
            if kind == "p":
                S.op("pool", lambda e: e.tensor_copy(raw[:, 0, 0:3], ctail[:, l, c, :]), reads=[("ctail", l, c)], writes=[rr])
            else:
                S.op("pool", lambda e: e.tensor_copy(raw[:, :, 0:3], sctail[:, :, c, :]), reads=[("sctail", l)], writes=[rr])
            S.op("act", lambda e: e.copy(raw[:, :, 3:3 + L], ps[:, 0:T].rearrange("p (s q) -> p s q", s=nseg, q=L)),
                 reads=[pres], writes=[rr])
            if kind == "p":
                S.op("pool", lambda e: e.tensor_copy(ctail[:, l, c, :], raw[:, 0, L:L + 3]), reads=[rr], writes=[("ctail", l, c)])
            else:
                S.op("pool", lambda e: e.tensor_copy(sctout[:, :, c, :], raw[:, :, L:L + 3]), reads=[rr], writes=[("sctout", l)])
            wc = lambda k: vfm[:, l, V_CW + k * 24 + c:V_CW + k * 24 + c + 1]
            S.op("dve", lambda e: e.tensor_scalar(acc[:, :, :], raw[:, :, 0:L], wc(0), vfm[:, l, V_CB + c:V_CB + c + 1],
                                                  op0=ALU.mult, op1=ALU.add), reads=[rr], writes=[ra])
            for k in range(1, 4):
                S.op("dve", lambda e, k=k: e.scalar_tensor_tensor(acc[:, :, :], raw[:, :, k:k + L], wc(k), acc[:, :, :],
                                                                  op0=ALU.mult, op1=ALU.add), reads=[rr, ra], writes=[ra])
            S.op("act", lambda e: e.activation(dst.rearrange("p (s q) -> p s q", s=nseg, q=L), acc[:, :, :], AF.Silu),
                 reads=[ra], writes=[dres])

        def ev_b(m, ps, pres):
            conv_chunk(16 + m, ps, pres, BT[:, m, 0:T], ("BT", m))
            for j in range(NT):
                pt, ptr = new_ps(bf=True)
                S.op("pe", lambda e: e.transpose(pt[0:Q, 0:128], BT[:, m, j * Q:(j + 1) * Q], ident_b[:, :]),
                     reads=[("BT", m), "ident_b"], writes=[ptr])
                S.op("dve", lambda e: e.tensor_copy(Btok[0:Q, j, m * 128:(m + 1) * 128], pt[0:Q, 0:128]),
                     reads=[ptr], writes=[("Btok", j, m)])
        linear_fm("ib", l, T, hact, hres, ev_b)

        def ev_c(m, ps, pres):
            conv_chunk(20 + m, ps, pres, CT[:, m, 0:T], ("CT", m))
        linear_fm("ic", l, T, hact, hres, ev_c)

        pcs = []
        for j in range(NT):
            pc, rc = new_ps()
            S.op("pe", lambda e, j=j, pc=pc: e.matmul(pc[0:Q, 0:32], U_f[0:Q, 0:Q], dta[0:Q, j, :], start=True, stop=True),
                 reads=[("dta", j), "U_f"], writes=[rc])
            pl, rl = new_ps()
            S.op("pe", lambda e, j=j, pl=pl: e.matmul(pl[:, 0:32], ones_f[0:Q, :], dta[0:Q, j, :], start=True, stop=True),
                 reads=[("dta", j), "ones_f"], writes=[rl])
            pcs.append((pc, rc, pl, rl))
        for j in range(NT):
            pc, rc, pl, rl = pcs[j]
            S.op("act", lambda e, j=j, pc=pc: e.activation(ec[0:Q, j, :], pc[0:Q, 0:32], AF.Exp), reads=[rc], writes=[("ec", j)])
            S.op("dve", lambda e, j=j, pc=pc: e.tensor_copy(cums[0:Q, j, :], pc[0:Q, 0:32]), reads=[rc], writes=[("cums", j)])
        for j in range(NT):
            pc, rc, pl, rl = pcs[j]
            S.op("dve", lambda e, j=j, pl=pl: e.tensor_tensor(de[0:Q, j, :], pl[0:Q, 0:32], cums[0:Q, j, :], ALU.subtract),
                 reads=[rl, ("cums", j)], writes=[("de", j)])
            S.op("act", lambda e, j=j, pl=pl: e.activation(eL[:, j, :], pl[:, 0:32], AF.Exp), reads=[rl], writes=[("eL", j)])
        for j in range(NT):
            S.op("act", lambda e, j=j: e.activation(de[0:Q, j, :], de[0:Q, j, :], AF.Exp), reads=[("de", j)], writes=[("de", j)])

        hpb = 512 // Q
        for g in range(G):
            gs = slice(g * 8, (g + 1) * 8)
            def ev_z(j, c0, w, ps, pres):
                S.op("act", lambda e: e.activation(zs[0:Q, j, c0 - g * 512:c0 - g * 512 + w], ps[0:Q, 0:w], AF.Silu),
                     reads=[pres], writes=[("zs", j, c0)])
            K = D
            for (c0, w) in [sl for sl in slabs_of("iz") if g * 512 <= sl[0] < (g + 1) * 512]:
                slab, sres = load_slab("iz", l, c0, w)
                for j in range(NT):
                    ps, pres = new_ps()
                    for k in range(KD):
                        S.op("pe", lambda e, k=k, ps=ps, j=j, slab=slab: e.matmul(
                            ps[0:Q, 0:w], hT[:, k, j * Q:(j + 1) * Q], slab[:, k, :], start=(k == 0), stop=(k == KD - 1)),
                            reads=[sres, ("hT", k)], writes=[pres])
                    ev_z(j, c0, w, ps, pres)
            for (c0, w) in [sl for sl in slabs_of("ix") if g * 512 <= sl[0] < (g + 1) * 512]:
                slab, sres = load_slab("ix", l, c0, w)
                for mi in range(w // 128):
                    m = c0 // 128 + mi
                    cc = m - g * 4
                    ps, pres = new_ps()
                    for k in range(KD):
                        S.op("pe", lambda e, k=k, ps=ps, mi=mi, slab=slab: e.matmul(
                            ps[:, 0:T], slab[:, k, mi * 128:(mi + 1) * 128], hT[:, k, 0:T], start=(k == 0), stop=(k == KD - 1)),
                            reads=[sres, ("hT", k)], writes=[pres])
                    conv_chunk(m, ps, pres, xc[:, m % 2, 0:T], ("xc", m % 2))
                    for j in range(NT):
                        pt, ptr = new_ps(bf=True)
                        S.op("pe", lambda e, j=j, pt=pt, m=m: e.transpose(pt[0:Q, 0:128], xc[:, m % 2, j * Q:(j + 1) * Q], ident_b[:, :]),
                             reads=[("xc", m % 2), "ident_b"], writes=[ptr])
                        S.op("dve", lambda e, j=j, pt=pt, cc=cc: e.tensor_copy(xs_tok[0:Q, j, cc * 128:(cc + 1) * 128], pt[0:Q, 0:128]),
                             reads=[ptr], writes=[("xs_tok", j, cc)])
            zres = lambda j: [("zs", j, c0) for (c0, w) in slabs_of("iz") if g * 512 <= c0 < (g + 1) * 512]
            xres = lambda j: [("xs_tok", j, cc) for cc in range(4)]
            if kind == "p":
                S.op("act", lambda e: e.copy(hstb[:, g * 512:(g + 1) * 512], hst[:, l, g * 512:(g + 1) * 512]),
                     reads=[("hst", l, g)], writes=[("hstb", g)])
            for j in range(NT):
                if kind == "s":
                    state_in(l, g, ss[l, j, g * 512:(g + 1) * 512, :])
                    S.op("act", lambda e: e.copy(hstb[:, g * 512:(g + 1) * 512], hst[:, l, g * 512:(g + 1) * 512]),
                         reads=[("hst", l, g)], writes=[("hstb", g)])
                jq = slice(j * Q, (j + 1) * Q)
                S.op("pool", lambda e: e.tensor_tensor(Wh[0:Q, :, 0:Q], bcx(SL_f[0:Q, 0:Q], [Q, 8, Q], 1),
                                                       bcx(dta[0:Q, j, gs], [Q, 8, Q], 2), ALU.mult),
                     reads=["SL_f", ("dta", j)], writes=["Wh"])
                for hb in range(8 // hpb):
                    ps_s, r_s = new_ps()
                    for hh in range(hpb):
                        h8 = hb * hpb + hh
                        S.op("pe", lambda e, hh=hh, h8=h8, ps_s=ps_s: e.matmul(ps_s[0:Q, hh * Q:(hh + 1) * Q], Wh[0:Q, h8, 0:Q], U_f[0:Q, 0:Q],
                                                                                 start=True, stop=True),
                             reads=["Wh", "U_f"], writes=[r_s])
                    S.op("act", lambda e, hb=hb, ps_s=ps_s: e.activation(
                        E[0:Q, hb * hpb:(hb + 1) * hpb, 0:Q], ps_s[0:Q, 0:hpb * Q].rearrange("p (h q) -> p h q", h=hpb, q=Q), AF.Exp),
                        reads=[r_s], writes=[("E", hb)])
                eres = [("E", hb) for hb in range(8 // hpb)]
                ps_cb, r_cb = new_ps()
                S.op("pe", lambda e: e.matmul(ps_cb[0:Q, 0:Q], BT[:, g, jq], CT[:, g, jq], start=True, stop=True),
                     reads=[("BT", g), ("CT", g)], writes=[r_cb])
                S.op("dve", lambda e: e.tensor_tensor(CBm[0:Q, 0:Q], ps_cb[0:Q, 0:Q], U_f[0:Q, 0:Q], ALU.mult),
                     reads=[r_cb, "U_f"], writes=["CBm"])
                S.op("dve", lambda e: e.tensor_tensor(Mm[0:Q, :, 0:Q], E[0:Q, :, 0:Q], bcx(CBm[0:Q, 0:Q], [Q, 8, Q], 1), ALU.mult),
                     reads=eres + ["CBm"], writes=["Mm"])
                S.op("pool", lambda e: e.tensor_tensor(xdt[0:Q, :].rearrange("p (h q) -> p h q", h=8, q=64),
                                                       xs_tok[0:Q, j, :].rearrange("p (h q) -> p h q", h=8, q=64),
                                                       bcx(dtt[0:Q, j, gs], [Q, 8, 64], 2), ALU.mult),
                     reads=xres(j) + [("dtt", j)], writes=["xdt"])
                S.op("pool", lambda e: e.tensor_tensor(xde[0:Q, :].rearrange("p (h q) -> p h q", h=8, q=64),
                                                       xdt[0:Q, :].rearrange("p (h q) -> p h q", h=8, q=64),
                                                       bcx(de[0:Q, j, gs], [Q, 8, 64], 2), ALU.mult),
                     reads=["xdt", ("de", j)], writes=["xde"])
                ps_y, r_y = new_ps()
                for h8 in range(8):
                    S.op("pe", lambda e, h8=h8: e.matmul(ps_y[0:Q, h8 * 64:(h8 + 1) * 64], Mm[0:Q, h8, 0:Q], xdt[0:Q, h8 * 64:(h8 + 1) * 64],
                                                          start=True, stop=True), reads=["Mm", "xdt"], writes=[r_y])
                ps_o, r_o = new_ps()
                S.op("pe", lambda e: e.matmul(ps_o[0:Q, 0:512], CT[:, g, jq], hstb[:, g * 512:(g + 1) * 512], start=True, stop=True),
                     reads=[("CT", g), ("hstb", g)], writes=[r_o])
                S.op("dve", lambda e: e.tensor_tensor(yo[0:Q, :].rearrange("p (h q) -> p h q", h=8, q=64),
                                                      ps_o[0:Q, 0:512].rearrange("p (h q) -> p h q", h=8, q=64),
                                                      bcx(ec[0:Q, j, gs], [Q, 8, 64], 2), ALU.mult),
                     reads=[r_o, ("ec", j)], writes=["yo"])
                S.op("dve", lambda e: e.tensor_tensor(yy[0:Q, :], ps_y[0:Q, 0:512], yo[0:Q, :], ALU.add),
                     reads=[r_y, "yo"], writes=["yy"])
                S.op("pool", lambda e: e.tensor_tensor(tmpd[0:Q, :].rearrange("p (h q) -> p h q", h=8, q=64),
                                                       xs_tok[0:Q, j, :].rearrange("p (h q) -> p h q", h=8, q=64),
                                                       bcx(hv[0:Q, l, 64 + g * 8:64 + (g + 1) * 8], [Q, 8, 64], 2), ALU.mult),
                     reads=xres(j) + [("hv", l)], writes=["tmpd"])
                S.op("dve", lambda e: e.tensor_tensor(yy[0:Q, :], yy[0:Q, :], tmpd[0:Q, :], ALU.add), reads=["yy", "tmpd"], writes=["yy"])
                S.op("dve", lambda e: e.tensor_tensor(yy[0:Q, :], yy[0:Q, :], zs[0:Q, j, :], ALU.mult), reads=["yy"] + zres(j), writes=["yy"])
                S.op("pool", lambda e: e.memset(small[0:Q, 16:17], 0.0), writes=["ssq"])
                S.op("act", lambda e: e.activation(tmpd[0:Q, :], yy[0:Q, :], AF.Square, accum_out=small[0:Q, 16:17]),
                     reads=["yy"], writes=["tmpd", "ssq"])
                S.op("act", lambda e: e.activation(small[0:Q, 17:18], small[0:Q, 16:17], AF.Ln, bias=EPS, scale=1.0 / 512),
                     reads=["ssq"], writes=["grs"])
                S.op("act", lambda e: e.activation(small[0:Q, 17:18], small[0:Q, 17:18], AF.Exp, scale=-0.5), reads=["grs"], writes=["grs"])
                S.op("dve", lambda e: e.tensor_scalar(yn[0:Q, :], yy[0:Q, :], small[0:Q, 17:18], None, op0=ALU.mult),
                     reads=["yy", "grs"], writes=["yn"])
                pt, ptr = new_ps(bf=True)
                for cc in range(4):
                    S.op("pe", lambda e, cc=cc: e.transpose(pt[:, cc * Q:(cc + 1) * Q], yn[0:Q, cc * 128:(cc + 1) * 128], ident_b[0:Q, 0:Q]),
                         reads=["yn", "ident_b"], writes=[ptr])
                S.op("dve", lambda e: e.tensor_tensor(yT[:, g * 4:(g + 1) * 4, jq], pt[:, 0:4 * Q].rearrange("p (c q) -> p c q", c=4, q=Q),
                                                      bcx(vfm[:, l, V_SSN + g * 4:V_SSN + (g + 1) * 4], [128, 4, Q], 2), ALU.mult),
                     reads=[ptr, ("vfm", l)], writes=[("yT", g, j)])
                ps_h, r_h = new_ps()
                S.op("pe", lambda e: e.matmul(ps_h[:, 0:512], Btok[0:Q, j, g * 128:(g + 1) * 128], xde[0:Q, :], start=True, stop=True),
                     reads=[("Btok", j, g), "xde"], writes=[r_h])
                hv3 = hst[:, l, g * 512:(g + 1) * 512].rearrange("p (h q) -> p h q", h=8, q=64)
                S.op("dve", lambda e: e.tensor_tensor(hv3, hv3, bcx(eL[:, j, gs], [128, 8, 64], 2), ALU.mult),
                     reads=[("hst", l, g), ("eL", j)], writes=[("hst", l, g)])
                S.op("dve", lambda e: e.tensor_tensor(hst[:, l, g * 512:(g + 1) * 512], hst[:, l, g * 512:(g + 1) * 512], ps_h[:, 0:512], ALU.add),
                     reads=[("hst", l, g), r_h], writes=[("hst", l, g)])
                if kind == "s":
                    state_out(l, g, ssm_s[l, j, g * 512:(g + 1) * 512, :])
                elif j < NT - 1:
                    S.op("act", lambda e: e.copy(hstb[:, g * 512:(g + 1) * 512], hst[:, l, g * 512:(g + 1) * 512]),
                         reads=[("hst", l, g)], writes=[("hstb", g)])

        dump("yT", yT[:, :, :], BF16)
        dump("hst", hst[:, :, :])
        S.barrier()
        P1 = carve(AB, (16, TP), F32)
        sgm = carve(AB + 32768, (2, TP), BF16)
        tmpm = carve(AB + 34816, (2, TP), F32)
        merged = yT
        yTres = lambda k: [("yT", k // 4, j) for j in range(NT)]
        for (c0, w) in slabs_of("wa"):
            slab_a, ra = load_slab("wa", l, c0, w)
            slab_g, rg = load_slab("iga", l, c0, w)
            for mi in range(w // 128):
                n = c0 // 128 + mi
                pa, rpa = new_ps()
                pg, rpg = new_ps()
                for k in range(KD):
                    S.op("pe", lambda e, k=k: e.matmul(pa[:, 0:T], slab_a[:, k, mi * 128:(mi + 1) * 128], yT[:, k, 0:T],
                                                        start=(k == 0), stop=(k == KD - 1)), reads=[ra] + yTres(k), writes=[rpa])
                for k in range(KD):
                    S.op("pe", lambda e, k=k: e.matmul(pg[:, 0:T], slab_g[:, k, mi * 128:(mi + 1) * 128], hT[:, k, 0:T],
                                                        start=(k == 0), stop=(k == KD - 1)), reads=[rg, ("hT", k)], writes=[rpg])
                b = n % 2
                S.op("act", lambda e: e.activation(sgm[:, b, 0:T], pg[:, 0:T], AF.Sigmoid), reads=[rpg], writes=[("sgm", b)])
                S.op("dve", lambda e: e.tensor_tensor(P1[:, n, 0:T], sgm[:, b, 0:T], pa[:, 0:T], ALU.mult),
                     reads=[("sgm", b), rpa], writes=[("P1", n)])
        S.barrier()
        for (c0, w) in slabs_of("wb"):
            slab_b, rb = load_slab("wb", l, c0, w)
            slab_g, rg = load_slab("igb", l, c0, w)
            for mi in range(w // 128):
                n = c0 // 128 + mi
                pb, rpb = new_ps()
                pg, rpg = new_ps()
                for k in range(8):
                    S.op("pe", lambda e, k=k: e.matmul(pb[:, 0:T], slab_b[:, k, mi * 128:(mi + 1) * 128], yb[:, k, 0:T],
                                                        start=(k == 0), stop=(k == 7)), reads=[rb, ("yb", k)], writes=[rpb])
                for k in range(KD):
                    S.op("pe", lambda e, k=k: e.matmul(pg[:, 0:T], slab_g[:, k, mi * 128:(mi + 1) * 128], hT[:, k, 0:T],
                                                        start=(k == 0), stop=(k == KD - 1)), reads=[rg, ("hT", k)], writes=[rpg])
                b = n % 2
                S.op("act", lambda e: e.activation(sgm[:, b, 0:T], pg[:, 0:T], AF.Sigmoid), reads=[rpg], writes=[("sgm", b)])
                S.op("dve", lambda e: e.tensor_tensor(tmpm[:, b, 0:T], sgm[:, b, 0:T], pb[:, 0:T], ALU.mult),
                     reads=[("sgm", b), rpb], writes=[("tmpm", b)])
                S.op("pool", lambda e: e.tensor_tensor(merged[:, n, 0:T], tmpm[:, b, 0:T], P1[:, n, 0:T], ALU.add),
                     reads=[("tmpm", b), ("P1", n)], writes=[("mrg", n)])

        dump("merged", merged[:, :, :], BF16)

        def ev_o(m, ps, pres):
            S.op("dve", lambda e: e.tensor_tensor(xT[:, m, 0:T], xT[:, m, 0:T], ps[:, 0:T], ALU.add),
                 reads=[pres, ("xT", m)], writes=[("xT", m)])
        linear_fm("wo", l, T, lambda k: merged[:, k, 0:T], lambda k: [("mrg", k)], ev_o)

    def ple(l, T, Q, NT, psrc):
        pst = [carve(i * 1024, (256,), F32) for i in range(2)]
        pT = carve(2048, (2, TP), BF16)
        ppw = carve(4096, (4, 2, 512), BF16)
        sgm = carve(12288, (2, TP), BF16)
        tmp = carve(14336, (2, TP), F32)
        S.dma("sp", "ppw", ppw[:, :, :, :].rearrange("p s k w -> p (s k w)"), scr[("pp", l)][:, :], reads=[("scr", "pp", l)], writes=["ppw"], after_bar=True)
        for j in range(NT):
            S.dma("sp", "pst%d" % (j % 2), pst[j % 2][0:Q, :], psrc[j * Q:(j + 1) * Q, :], writes=[("pst", j % 2)], after_bar=True)
            ps, pres = new_ps()
            for k in range(2):
                S.op("pe", lambda e, k=k, ps=ps: e.transpose(ps[:, k * Q:(k + 1) * Q], pst[j % 2][0:Q, k * 128:(k + 1) * 128], ident_f[0:Q, 0:Q]),
                     reads=[("pst", j % 2), "ident_f"], writes=[pres])
            S.op("dve", lambda e, ps=ps: e.tensor_copy(pT[:, :, j * Q:(j + 1) * Q], ps[:, 0:2 * Q].rearrange("p (k q) -> p k q", k=2, q=Q)),
                 reads=[pres], writes=[("pT", j)])
        rmsnorm_to_hT(vfm[:, l, V_PLN:V_PLN + KD], T)
        pTres = [("pT", j) for j in range(NT)]

        def ev(m, ps, pres):
            pp_, rpp = new_ps()
            for k in range(2):
                S.op("pe", lambda e, k=k: e.matmul(pp_[:, 0:T], ppw[:, m // 4, k, (m % 4) * 128:(m % 4 + 1) * 128], pT[:, k, 0:T], start=(k == 0), stop=(k == 1)),
                     reads=["ppw"] + pTres, writes=[rpp])
            b = m % 2
            S.op("act", lambda e: e.activation(sgm[:, b, 0:T], ps[:, 0:T], AF.Sigmoid), reads=[pres], writes=[("sgm", b)])
            S.op("dve", lambda e: e.tensor_tensor(tmp[:, b, 0:T], sgm[:, b, 0:T], pp_[:, 0:T], ALU.mult),
                 reads=[("sgm", b), rpp], writes=[("ptmp", b)])
            S.op("pool", lambda e: e.tensor_tensor(xT[:, m, 0:T], xT[:, m, 0:T], tmp[:, b, 0:T], ALU.add),
                 reads=[("ptmp", b), ("xT", m)], writes=[("xT", m)])
        linear_fm("pg", l, T, lambda k: hT[:, k, 0:T], lambda k: [("hT", k)], ev)

    tiles = [("p", i) for i in range(n_ptiles)]
    if with_sample:
        tiles.append(("s", 0))
    ost = {"n": 0}
    ps_tmp = nc.alloc_sbuf_tensor("ps_tmp", [128, TP], F32)
    sctout = nc.alloc_sbuf_tensor("sctout", [128, 4, 24, 3], F32)
    S.op("pool", lambda e: e.memset(hst[:, :, :], 0.0), writes=[("hst", l, g) for l in range(DEPTH) for g in range(G)])
    S.op("pool", lambda e: e.memset(ctail[:, :, :, :], 0.0), writes=[("ctail", l, c) for l in range(DEPTH) for c in range(24)])

    for (kind, ti) in tiles:
        if kind == "p":
            T, Q, NT = TP, 128, TP // 128
            xsrc = xp[ti * TP:(ti + 1) * TP, :]
            ydst = yp[ti * TP:(ti + 1) * TP, :]
        else:
            T, Q, NT = 256, 64, 4
            xsrc = xs
            ydst = ys
        bar = S.barrier()
        for j in range(NT):
            xin = carve(A_XIN + (j % 2) * 8192, (KD, 128), F32)
            S.dma("sp", "xin%d" % (j % 2), xin[0:Q], xsrc[j * Q:(j + 1) * Q, :].rearrange("t (k c) -> t k c", k=KD),
                  writes=[("xio", j % 2)], after_bar=True)
            for k4 in range(KD // 4):
                ps, pres = new_ps()
                for kk in range(4):
                    k = k4 * 4 + kk
                    S.op("pe", lambda e, ps=ps, kk=kk, k=k, xin=xin: e.transpose(
                        ps[:, kk * 128:kk * 128 + Q], xin[0:Q, k, :], ident_f[0:Q, 0:Q]),
                        reads=[("xio", j % 2), "ident_f"], writes=[pres])
                en = "act" if k4 % 2 else "dve"
                fn = (lambda e, ps=ps, k4=k4, j=j: e.copy(
                    xT[:, k4 * 4:(k4 + 1) * 4, j * Q:(j + 1) * Q], ps[:, :].rearrange("p (a b) -> p a b", a=4, b=128)[:, :, 0:Q])) \
                    if en == "act" else (lambda e, ps=ps, k4=k4, j=j: e.tensor_copy(
                        xT[:, k4 * 4:(k4 + 1) * 4, j * Q:(j + 1) * Q], ps[:, :].rearrange("p (a b) -> p a b", a=4, b=128)[:, :, 0:Q]))
                S.op(en, fn, reads=[pres], writes=[("xT", k4 * 4 + i) for i in range(4)])

        stopped = False
        for l in range(layers):
            if l not in cast_done:
                cast_weights(l, order)
                cast_done.add(l)
            if kind == "s":
                for sq_ in range(4):
                    cs = carve(0, (3072,), F32)
                    S.dma("sp", "scin", cs[0:3, :], sc[l, sq_], writes=["scin"], after_bar=True)
                    for c4 in range(6):
                        ps, pres = new_ps()
                        for cc in range(4):
                            c = c4 * 4 + cc
                            S.op("pe", lambda e, c=c, cc=cc, ps=ps, cs=cs: e.transpose(ps[:, cc * 3:(cc + 1) * 3], cs[0:3, c * 128:(c + 1) * 128], ident_f[0:3, 0:3]),
                                 reads=["scin", "ident_f"], writes=[pres])
                        S.op("dve", lambda e, ps=ps, c4=c4, sq_=sq_: e.tensor_copy(sctail[:, sq_, c4 * 4:(c4 + 1) * 4, :],
                                                                                    ps[:, 0:12].rearrange("p (c k) -> p c k", c=4, k=3)),
                             reads=[pres], writes=[("sctail", l)])
                S.barrier()
            ffn(l, 1, T, V_F1N)
            if stop_after == (l, "ffn1"):
                stopped = True
                break
            S.barrier()
            mixer(l, T, Q, NT, kind, ti)
            if stop_after == (l, "mixer"):
                stopped = True
                break
            S.barrier()
            ffn(l, 2, T, V_F2N)
            if stop_after == (l, "ffn2"):
                stopped = True
                break
            S.barrier()
            ple(l, T, Q, NT, (ppr[l, ti * TP:(ti + 1) * TP, :] if kind == "p" else psm[l]))
            if stop_after == (l, "ple"):
                stopped = True
                break
            S.barrier()
            if kind == "s":
                for sq_ in range(4):
                    tails_out(lambda c, sq_=sq_: sctout[:, sq_, c, :], conv_s[l, sq_])
                S.barrier()

        bar = S.barrier()
        if not stopped:
            compute_rstd(T)
            for k in range(KD):
                S.op("dve", lambda e, k=k: e.scalar_tensor_tensor(
                    xT[:, k, 0:T], xT[:, k, 0:T], fng[:, k:k + 1], rstd[:, 0:T], op0=ALU.mult, op1=ALU.mult),
                    reads=[("xT", k), "rstd", "fng"], writes=[("xT", k)])
        for j in range(NT):
            yo_ = carve(A_XIN + (j % 2) * 8192, (KD, 128), F32)
            for k4 in range(KD // 4):
                ps, pres = new_ps()
                for kk in range(4):
                    k = k4 * 4 + kk
                    S.op("pe", lambda e, ps=ps, kk=kk, k=k, j=j: e.transpose(
                        ps[0:Q, kk * 128:(kk + 1) * 128], xT[:, k, j * Q:(j + 1) * Q], ident_f[:, :]),
                        reads=[("xT", k), "ident_f"], writes=[pres])
                S.op("dve", lambda e, ps=ps, k4=k4, yo_=yo_: e.tensor_copy(
                    yo_[0:Q, k4 * 4:(k4 + 1) * 4, :], ps[0:Q, :].rearrange("p (a b) -> p a b", a=4, b=128)),
                    reads=[pres], writes=[("xio", j % 2)])
            S.dma("pool", "yo%d" % (j % 2), ydst[j * Q:(j + 1) * Q, :].rearrange("t (k c) -> t k c", k=KD), yo_[0:Q],
                  reads=[("xio", j % 2)], arena_read=True)
        if kind == "p" and ti == n_ptiles - 1 and not stopped:
            S.barrier()
            for l in range(layers):
                for g in range(G):
                    state_out(l, g, ssm_p[l, g * 512:(g + 1) * 512, :])
                tails_out(lambda c, l=l: ctail[:, l, c, :], conv_p[l])
            S.barrier()
    S.finish()
    return nc, S


def _fm(v, k):
    return np.ascontiguousarray(np.asarray(v, np.float32).reshape(k, 128).T)


ACTIVE = [0, 1, 4, 5]


def make_in_maps(inp, n_cores=8):
    vecs = np.zeros((DEPTH, 128, NV), np.float32)
    for l in range(DEPTH):
        vecs[l, :, V_F1N:V_F1N + 16] = _fm(inp["ffn1_norm"][l], 16)
        vecs[l, :, V_MXN:V_MXN + 16] = _fm(inp["mix_norm"][l], 16)
        vecs[l, :, V_F2N:V_F2N + 16] = _fm(inp["ffn2_norm"][l], 16)
        vecs[l, :, V_PLN:V_PLN + 16] = _fm(inp["ple_norm"][l], 16)
        vecs[l, :, V_SSN:V_SSN + 16] = _fm(inp["ssd_norm"][l], 16)
        vecs[l, :, V_CB:V_CB + 24] = _fm(inp["conv_b"][l], 24)
        for k in range(4):
            vecs[l, :, V_CW + k * 24:V_CW + (k + 1) * 24] = _fm(inp["conv_w"][l][k], 24)
    hvec = np.concatenate([np.asarray(inp["dt_bias"]), np.asarray(inp["a_log"]), np.asarray(inp["d_skip"])],
                          axis=1).astype(np.float32).reshape(DEPTH, 1, 96)
    shared = {n: np.ascontiguousarray(np.asarray(inp[n], np.float32)) for n in WSRC}
    shared["vecs_fm"] = vecs
    shared["fnorm_fm"] = _fm(inp["final_norm"], 16)
    shared["hvec"] = hvec
    shared["gm_ln_g"] = np.asarray(inp["gm_ln_g"], np.float32).reshape(DEPTH, 1, GMW)
    shared["gm_ln_b"] = np.asarray(inp["gm_ln_b"], np.float32).reshape(DEPTH, 1, GMW)
    shared["gm_wsT"] = np.ascontiguousarray(np.transpose(np.asarray(inp["gm_ws"], np.float32), (0, 3, 1, 2)))
    shared["gm_bs"] = np.asarray(inp["gm_bs"], np.float32).reshape(DEPTH, 1, 8 * 128)
    maps = []
    zero_shared = None
    for c in range(n_cores):
        if n_cores == 8 and c not in ACTIVE:
            if zero_shared is None:
                zero_shared = {k: np.zeros_like(v) for k, v in shared.items()}
                zero_shared.update({"xp": np.zeros((SEQ, D), np.float32), "ppr": np.zeros((DEPTH, SEQ, PLE), np.float32),
                                    "xs": np.zeros((256, D), np.float32), "psm": np.zeros((DEPTH, 256, PLE), np.float32),
                                    "sc": np.zeros((DEPTH, 4, 3, XBC), np.float32),
                                    "ss": np.zeros((DEPTH, 4, H * HP, NS), np.float32)})
            maps.append(zero_shared)
            continue
        b = ACTIVE.index(c) if n_cores == 8 else c % 4
        m = dict(shared)
        m["xp"] = np.ascontiguousarray(np.asarray(inp["x_prompt"][b], np.float32))
        m["ppr"] = np.ascontiguousarray(np.asarray(inp["p_prompt"][:, b], np.float32))
        m["xs"] = np.ascontiguousarray(np.asarray(inp["x_sample"][4 * b:4 * b + 4], np.float32).reshape(256, D))
        m["psm"] = np.ascontiguousarray(np.asarray(inp["p_sample"][:, 4 * b:4 * b + 4], np.float32).reshape(DEPTH, 256, PLE))
        m["sc"] = np.ascontiguousarray(np.asarray(inp["state_conv"][:, 4 * b:4 * b + 4], np.float32))
        m["ss"] = np.ascontiguousarray(np.asarray(inp["state_ssm"][:, 4 * b:4 * b + 4], np.float32).reshape(DEPTH, 4, H * HP, NS))
        maps.append(m)
    return maps


_CACHE = {}


def kernel(**inputs):
    if "nc" not in _CACHE:
        _CACHE["nc"] = build()[0]
    nc = _CACHE["nc"]
    maps = make_in_maps(inputs, 8)
    allres = run_bass_kernel_spmd(nc, maps, core_ids=list(range(8))).results
    res = [allres[c] for c in ACTIVE]
    B, DB = 4, 16
    y_prompt = np.stack([res[b]["yp"] for b in range(B)])
    y_sample = np.concatenate([res[b]["ys"].reshape(4, 64, D) for b in range(B)])
    ssm_prompt = np.stack([res[b]["ssm_p"].reshape(DEPTH, H, HP, NS) for b in range(B)], axis=1)
    conv_prompt = np.stack([res[b]["conv_p"] for b in range(B)], axis=1)
    ssm_sample = np.concatenate([res[b]["ssm_s"].reshape(DEPTH, 4, H, HP, NS) for b in range(B)], axis=1)
    conv_sample = np.concatenate([res[b]["conv_s"] for b in range(B)], axis=1)
    gv = np.concatenate([res[b]["gv"].reshape(DEPTH, 4, 64, GMW) for b in range(B)], axis=1)
    f = lambda a: np.ascontiguousarray(a, dtype=np.float32)
    return (f(y_prompt), f(y_sample), f(ssm_prompt), f(conv_prompt), f(ssm_sample), f(conv_sample), f(gv))
```
